# Optimizing a Trainium2 kernel written in Bass

```python
import math
import jax, jax.numpy as jnp
from jax import lax
import numpy as np

D_MODEL = 1024
BATCH = 8
SEQ = 4096
DEPTH = 2
DEC_BATCH = 8
DEC_SEQ = 2048
PAST_LEN = 128

POOL_WINDOWS = (2, 4, 8, 16)
N_POOL_GROUPS = 4
POOL_GROUP_DIM = D_MODEL // 8
POOL_WIDTH = N_POOL_GROUPS * POOL_GROUP_DIM
SGU_HEADS = 4
SGU_HEAD_DIM = D_MODEL // 8
SGU_WIDTH = SGU_HEADS * SGU_HEAD_DIM
CHUNK = 128
IN_WIDTH = POOL_WIDTH + 2 * SGU_WIDTH
MIX_WIDTH = POOL_WIDTH + SGU_WIDTH
CONV_WIDTH = D_MODEL
CONV_TAPS = 31
D_FF = 4 * D_MODEL
N_MOD = 6
ALPHA = (2.0 * DEPTH) ** 0.25
BETA = (8.0 * DEPTH) ** -0.25
LN_EPS = 1e-5

kernel_name = "hybrid_pool_sgu_conformer_encoder"


def layer_norm(x, g, b):
    xf = x.astype(jnp.float32)
    mu = jnp.mean(xf, axis=-1, keepdims=True)
    var = jnp.mean(jnp.square(xf - mu), axis=-1, keepdims=True)
    return ((xf - mu) * lax.rsqrt(var + LN_EPS) * g.astype(jnp.float32) + b.astype(jnp.float32)).astype(x.dtype)


def adaln(c, w, b):
    mod = jax.nn.silu(c) @ w + b
    return jnp.split(mod, N_MOD, axis=-1)


def modulate(x, shift, scale):
    return x * (1.0 + scale[:, None, :]) + shift[:, None, :]


def post_norm_residual(x, y, gate, g, b):
    return layer_norm(ALPHA * x + gate[:, None, :] * y, g, b)


def centred_mean_minus_identity(a, window):
    s = a.shape[1]
    af = a.astype(jnp.float32)
    cs = jnp.concatenate([jnp.zeros_like(af[:, :1]), jnp.cumsum(af, axis=1)], axis=1)
    t = jnp.arange(s)
    lo = jnp.clip(t - window // 2, 0, s)
    hi = jnp.clip(t + window // 2, 0, s)
    total = jnp.take(cs, hi, axis=1) - jnp.take(cs, lo, axis=1)
    cnt = (hi - lo).astype(jnp.float32)[None, :, None]
    return (total / cnt - af).astype(a.dtype)


def pool_mixer(a, pool_w, pool_scale):
    b, s, _ = a.shape
    groups = jnp.split(a, N_POOL_GROUPS, axis=-1)
    pooled = jnp.stack([centred_mean_minus_identity(gx, w) for gx, w in zip(groups, POOL_WINDOWS)], axis=2)
    mixed = jnp.einsum('bsgd,gde->bsge', pooled, pool_w).reshape(b, s, POOL_WIDTH)
    return mixed * pool_scale


def spatial_gating(u, v, ln_g, ln_b, sgu_w, sgu_b):
    b, s, _ = u.shape
    v = layer_norm(v, ln_g, ln_b)
    vc = v.reshape(b, s // CHUNK, CHUNK, SGU_HEADS, SGU_HEAD_DIM)
    vm = jnp.einsum('hpq,bnqhd->bnphd', sgu_w, vc) + sgu_b.T[:, :, None]
    return u * vm.reshape(b, s, SGU_WIDTH)


def channel_mixer(x, shift, scale, gate, w1, w2, g, b):
    h = modulate(x, shift, scale)
    y = jnp.square(jax.nn.relu(h @ w1)) @ w2
    return post_norm_residual(x, y, gate, g, b)


def even_layer(x, c, ada_w, ada_b, in_w, pool_w, pool_scale, sgu_ln_g, sgu_ln_b, sgu_w, sgu_b,
               out_w, ln1_g, ln1_b, mlp_w1, mlp_w2, ln2_g, ln2_b):
    sh_m, sc_m, gt_m, sh_f, sc_f, gt_f = adaln(c, ada_w, ada_b)
    h = modulate(x, sh_m, sc_m)
    z = h @ in_w
    a = z[..., :POOL_WIDTH]
    uv = jax.nn.gelu(z[..., POOL_WIDTH:], approximate=False)
    u, v = uv[..., :SGU_WIDTH], uv[..., SGU_WIDTH:]
    y_a = pool_mixer(a, pool_w, pool_scale)
    y_b = spatial_gating(u, v, sgu_ln_g, sgu_ln_b, sgu_w, sgu_b)
    y = jnp.concatenate([y_a, y_b], axis=-1) @ out_w
    x = post_norm_residual(x, y, gt_m, ln1_g, ln1_b)
    return channel_mixer(x, sh_f, sc_f, gt_f, mlp_w1, mlp_w2, ln2_g, ln2_b)


def odd_layer(x, c, ada_w, ada_b, pw1_w, pw1_b, dw_w, dw_b, cnorm_g, cnorm_b, pw2_w, pw2_b,
              ln1_g, ln1_b, mlp_w1, mlp_w2, ln2_g, ln2_b):
    sh_m, sc_m, gt_m, sh_f, sc_f, gt_f = adaln(c, ada_w, ada_b)
    h = modulate(x, sh_m, sc_m)
    p = h @ pw1_w + pw1_b
    g = p[..., :CONV_WIDTH] * jax.nn.sigmoid(p[..., CONV_WIDTH:])
    pad = CONV_TAPS // 2
    d = lax.conv_general_dilated(g, dw_w[:, None, :].astype(g.dtype), window_strides=(1,),
                                 padding=[(pad, pad)], dimension_numbers=('NWC', 'WIO', 'NWC'),
                                 feature_group_count=CONV_WIDTH) + dw_b
    d = jax.nn.silu(layer_norm(d, cnorm_g, cnorm_b))
    y = d @ pw2_w + pw2_b
    x = post_norm_residual(x, y, gt_m, ln1_g, ln1_b)
    return channel_mixer(x, sh_f, sc_f, gt_f, mlp_w1, mlp_w2, ln2_g, ln2_b)


def trunk(x, c, even_params, odd_params):
    for i in range(DEPTH):
        if i % 2 == 0:
            x = even_layer(x, c, *even_params)
        else:
            x = odd_layer(x, c, *odd_params)
    return x


def setup_inputs(seed: int = 0) -> dict:
    key = jax.random.key(seed)
    ks = iter(jax.random.split(key, 48))

    def nrm(shape, scale):
        return jax.random.normal(next(ks), shape, jnp.float32) * scale

    def gain(n):
        return 1.0 + nrm((n,), 0.02)

    def bias(shape):
        return nrm(shape, 0.02)

    d = D_MODEL
    return {
        "x_prompt": nrm((BATCH, SEQ, d), 1.0),
        "x_sample": nrm((DEC_BATCH, DEC_SEQ, d), 1.0),
        "c_prompt": nrm((BATCH, d), 1.0),
        "c_sample": nrm((DEC_BATCH, d), 1.0),
        "l0_ada_w": nrm((d, N_MOD * d), 0.5 * d ** -0.5),
        "l0_ada_b": bias((N_MOD * d,)),
        "l0_in_w": nrm((d, IN_WIDTH), d ** -0.5),
        "l0_pool_w": nrm((N_POOL_GROUPS, POOL_GROUP_DIM, POOL_GROUP_DIM), POOL_GROUP_DIM ** -0.5),
        "l0_pool_scale": gain(POOL_WIDTH),
        "l0_sgu_ln_g": gain(SGU_WIDTH),
        "l0_sgu_ln_b": bias((SGU_WIDTH,)),
        "l0_sgu_w": nrm((SGU_HEADS, CHUNK, CHUNK), CHUNK ** -0.5),
        "l0_sgu_b": 1.0 + nrm((SGU_HEADS, CHUNK), 0.02),
        "l0_out_w": nrm((MIX_WIDTH, d), BETA * MIX_WIDTH ** -0.5),
        "l0_ln1_g": gain(d),
        "l0_ln1_b": bias((d,)),
        "l0_mlp_w1": nrm((d, D_FF), d ** -0.5),
        "l0_mlp_w2": nrm((D_FF, d), BETA * D_FF ** -0.5),
        "l0_ln2_g": gain(d),
        "l0_ln2_b": bias((d,)),
        "l1_ada_w": nrm((d, N_MOD * d), 0.5 * d ** -0.5),
        "l1_ada_b": bias((N_MOD * d,)),
        "l1_pw1_w": nrm((d, 2 * CONV_WIDTH), d ** -0.5),
        "l1_pw1_b": bias((2 * CONV_WIDTH,)),
        "l1_dw_w": nrm((CONV_TAPS, CONV_WIDTH), CONV_TAPS ** -0.5),
        "l1_dw_b": bias((CONV_WIDTH,)),
        "l1_cnorm_g": gain(CONV_WIDTH),
        "l1_cnorm_b": bias((CONV_WIDTH,)),
        "l1_pw2_w": nrm((CONV_WIDTH, d), BETA * CONV_WIDTH ** -0.5),
        "l1_pw2_b": bias((d,)),
        "l1_ln1_g": gain(d),
        "l1_ln1_b": bias((d,)),
        "l1_mlp_w1": nrm((d, D_FF), d ** -0.5),
        "l1_mlp_w2": nrm((D_FF, d), BETA * D_FF ** -0.5),
        "l1_ln2_g": gain(d),
        "l1_ln2_b": bias((d,)),
    }


def reference(x_prompt, x_sample, c_prompt, c_sample,
              l0_ada_w, l0_ada_b, l0_in_w, l0_pool_w, l0_pool_scale, l0_sgu_ln_g, l0_sgu_ln_b,
              l0_sgu_w, l0_sgu_b, l0_out_w, l0_ln1_g, l0_ln1_b, l0_mlp_w1, l0_mlp_w2, l0_ln2_g, l0_ln2_b,
              l1_ada_w, l1_ada_b, l1_pw1_w, l1_pw1_b, l1_dw_w, l1_dw_b, l1_cnorm_g, l1_cnorm_b,
              l1_pw2_w, l1_pw2_b, l1_ln1_g, l1_ln1_b, l1_mlp_w1, l1_mlp_w2, l1_ln2_g, l1_ln2_b):
    even_params = (l0_ada_w, l0_ada_b, l0_in_w, l0_pool_w, l0_pool_scale, l0_sgu_ln_g, l0_sgu_ln_b,
                   l0_sgu_w, l0_sgu_b, l0_out_w, l0_ln1_g, l0_ln1_b, l0_mlp_w1, l0_mlp_w2, l0_ln2_g, l0_ln2_b)
    odd_params = (l1_ada_w, l1_ada_b, l1_pw1_w, l1_pw1_b, l1_dw_w, l1_dw_b, l1_cnorm_g, l1_cnorm_b,
                  l1_pw2_w, l1_pw2_b, l1_ln1_g, l1_ln1_b, l1_mlp_w1, l1_mlp_w2, l1_ln2_g, l1_ln2_b)
    y_prompt = trunk(x_prompt, c_prompt, even_params, odd_params)
    y_sample = trunk(x_sample, c_sample, even_params, odd_params)
    return (y_prompt, y_sample)
```

```python
import contextlib
import math
import numpy as np
import ml_dtypes
import concourse.bass as bass
import concourse.mybir as mybir
from concourse.bass_utils import run_bass_kernel_spmd

F32 = mybir.dt.float32
BF16 = mybir.dt.bfloat16
ALU = mybir.AluOpType
AF = mybir.ActivationFunctionType

D = 1024
KC = 8
T = 512
DFF = 4096
DEPTH = 2
ALPHA = (2.0 * DEPTH) ** 0.25
LN_EPS = 1e-5
EPS_POST = LN_EPS / (ALPHA * ALPHA)
POOL_WINDOWS = (2, 4, 8, 16)
TAPS = 31
HALO0 = 8
HALO1 = 15
N_DMA_SEMS = 24
N_WSLOTS = 3

WEIGHT_NAMES = [
    "l0_ada_w", "l0_ada_b", "l0_in_w", "l0_pool_w", "l0_pool_scale", "l0_sgu_ln_g", "l0_sgu_ln_b",
    "l0_sgu_w", "l0_sgu_b", "l0_out_w", "l0_ln1_g", "l0_ln1_b", "l0_mlp_w1", "l0_mlp_w2", "l0_ln2_g",
    "l0_ln2_b",
    "l1_ada_w", "l1_ada_b", "l1_pw1_w", "l1_pw1_b", "l1_dw_w", "l1_dw_b", "l1_cnorm_g", "l1_cnorm_b",
    "l1_pw2_w", "l1_pw2_b", "l1_ln1_g", "l1_ln1_b", "l1_mlp_w1", "l1_mlp_w2", "l1_ln2_g", "l1_ln2_b",
]


class _Rec:
    def __init__(self):
        self.call = None

    def __getattr__(self, name):
        def f(*a, **kw):
            self.call = (name, a, kw)
            return self
        return f


class Sched:
    def __init__(self, nc, stack):
        self.nc = nc
        self.engs = ["pe", "act", "dve", "pool", "sp"]
        self.ops = {e: [] for e in self.engs}
        self.sem = {e: stack.enter_context(nc.semaphore("s_" + e)) for e in self.engs}
        self.count = {e: 0 for e in self.engs}
        self.dsem = [stack.enter_context(nc.semaphore("d%d" % i)) for i in range(N_DMA_SEMS)]
        self.dval = [0] * N_DMA_SEMS
        self.dnext = 0
        self.waited = {e: {} for e in self.engs}
        self.regions = {}
        self.n_waits = 0
        self.n_ops = {e: 0 for e in self.engs}

    def _reg(self, r):
        if r not in self.regions:
            self.regions[r] = {"w": None, "r": {}}
        return self.regions[r]

    def _need(self, eng, reads, writes):
        need = {}

        def add(tok):
            if tok is None:
                return
            k, v = tok
            if k == ("e", "pe") and eng == "pe":
                return
            if need.get(k, 0) < v:
                need[k] = v

        for r in reads:
            add(self._reg(r)["w"])
        for w in writes:
            rg = self._reg(w)
            add(rg["w"])
            for k, v in rg["r"].items():
                if k == ("e", eng):
                    continue
                add((k, v))
        return need

    def _emit_waits(self, eng, need):
        for k, v in need.items():
            if self.waited[eng].get(k, 0) >= v:
                continue
            self.waited[eng][k] = v
            semh = self.sem[k[1]] if k[0] == "e" else self.dsem[k[1]]
            self.ops[eng].append(lambda e, semh=semh, v=v: e.wait_ge(semh, v))
            self.n_waits += 1

    def _commit(self, tok, reads, writes):
        k, v = tok
        for r in reads:
            self._reg(r)["r"][k] = v
        for w in writes:
            rg = self._reg(w)
            rg["w"] = tok
            rg["r"] = {}

    def op(self, eng, fns, reads=(), writes=()):
        if callable(fns):
            fns = [fns]
        calls = []
        for f in fns:
            r = _Rec()
            f(r)
            assert r.call is not None
            calls.append(r.call)
        bank_rd = [r for r in reads if isinstance(r, tuple) and r[0] == "bank"]
        if bank_rd:
            writes = list(writes) + bank_rd
        need = self._need(eng, reads, writes)
        self._emit_waits(eng, need)
        self.count[eng] += 1
        v = self.count[eng]
        semh = self.sem[eng]
        for (name, a, kw) in calls[:-1]:
            self.ops[eng].append(lambda e, name=name, a=a, kw=kw: getattr(e, name)(*a, **kw))
        name, a, kw = calls[-1]
        self.ops[eng].append(lambda e, name=name, a=a, kw=kw, semh=semh: getattr(e, name)(*a, **kw).then_inc(semh, 1))
        self.n_ops[eng] += len(fns)
        self._commit((("e", eng), v), reads, writes)

    def dma_once(self, q, out, in_, stack, reads=(), writes=(), **kw):
        i = len(self.dsem)
        self.dsem.append(stack.enter_context(self.nc.semaphore("c%d" % i)))
        self.dval.append(0)
        need = self._need(q, reads, writes)
        self._emit_waits(q, need)
        self.dval[i] = 16
        semh = self.dsem[i]
        self.ops[q].append(lambda e, out=out, in_=in_, semh=semh, kw=kw:
                           e.dma_start(out=out, in_=in_, **kw).then_inc(semh, 16))
        self._commit((("d", i), 16), reads, writes)

    def dma(self, q, out, in_, reads=(), writes=(), **kw):
        i = self.dnext
        self.dnext = (self.dnext + 1) % N_DMA_SEMS
        need = self._need(q, reads, writes)
        if self.dval[i] > 0:
            k = ("d", i)
            if need.get(k, 0) < self.dval[i]:
                need[k] = self.dval[i]
        self._emit_waits(q, need)
        self.dval[i] += 16
        v = self.dval[i]
        semh = self.dsem[i]
        self.ops[q].append(lambda e, out=out, in_=in_, semh=semh, kw=kw:
                           e.dma_start(out=out, in_=in_, **kw).then_inc(semh, 16))
        self._commit((("d", i), v), reads, writes)

    def barrier(self):
        need = {}
        for i in range(len(self.dval)):
            if self.dval[i] > 0:
                need[("d", i)] = self.dval[i]
        for e in self.engs:
            if self.count[e] > 0:
                need[("e", e)] = self.count[e]
        for e in self.engs:
            n2 = {k: v for k, v in need.items() if k != ("e", e)}
            self._emit_waits(e, n2)

    def finish(self):
        need = {}
        for i in range(len(self.dval)):
            if self.dval[i] > 0:
                need[("d", i)] = self.dval[i]
        for e in ["pe", "act", "dve", "pool"]:
            if self.count[e] > 0:
                need[("e", e)] = self.count[e]
        self._emit_waits("sp", need)

    def emit(self):
        nc = self.nc
        ops = self.ops
        with nc.Block() as block:
            @block.sync
            def _(e):
                for f in ops["sp"]:
                    f(e)

            @block.tensor
            def _(e):
                for f in ops["pe"]:
                    f(e)

            @block.scalar
            def _(e):
                for f in ops["act"]:
                    f(e)

            @block.vector
            def _(e):
                for f in ops["dve"]:
                    f(e)

            @block.gpsimd
            def _(e):
                for f in ops["pool"]:
                    f(e)


def _pool_mats(S):
    out = np.zeros((5, 128, 4, 128), np.float32)
    big = 1 << 20

    def fill(dst, tin, tout, S_):
        for g, w in enumerate(POOL_WINDOWS):
            lo = np.clip(tout - w // 2, 0, S_)
            hi = np.clip(tout + w // 2, 0, S_)
            cnt = (hi - lo).astype(np.float32)
            ti = tin[:, None]
            inside = (ti >= lo[None, :]) & (ti < hi[None, :]) & (ti >= 0) & (ti < S_)
            m = inside.astype(np.float32) / cnt[None, :]
            m -= ((ti == tout[None, :]) & (ti >= 0) & (ti < S_)).astype(np.float32)
            dst[:, g, :] = m

    r = np.arange(128)
    c = np.arange(128)
    base = 4096
    fill(out[0], base - 8 + r, base + c, big)
    fill(out[3], base + 120 + r, base + c, big)
    out[3][16:] = 0.0
    fill(out[1], r - 8, c, big)
    fill(out[2], S - 128 - 8 + r, S - 128 + c, S)
    fill(out[4], S - 8 + r, S - 128 + c, S)
    out[4][16:] = 0.0
    return out.astype(ml_dtypes.bfloat16)


def _pool_resid(S):
    out = np.zeros((128, 2, 4, 3, 8), np.float32)
    big = 1 << 20
    r = np.arange(128)
    c = np.arange(128)

    def fullmat(tin, tout, S_):
        m = np.zeros((128, 4, 128), np.float32)
        for g, w in enumerate(POOL_WINDOWS):
            lo = np.clip(tout - w // 2, 0, S_)
            hi = np.clip(tout + w // 2, 0, S_)
            cnt = (hi - lo).astype(np.float32)
            ti = tin[:, None]
            inside = (ti >= lo[None, :]) & (ti < hi[None, :]) & (ti >= 0) & (ti < S_)
            mm = inside.astype(np.float32) / cnt[None, :]
            mm -= ((ti == tout[None, :]) & (ti >= 0) & (ti < S_)).astype(np.float32)
            m[:, g, :] = mm
        return m

    mats = [
        (fullmat(r - 8, c, big), slice(0, 8)),
        (fullmat(S - 128 - 8 + r, S - 128 + c, S), slice(120, 128)),
        (fullmat(S - 8 + r, S - 128 + c, S), slice(120, 128)),
    ]
    mats[2][0][16:] = 0.0
    bf = lambda a: a.astype(ml_dtypes.bfloat16).astype(np.float32)
    for v, (m, cs) in enumerate(mats):
        res = m - bf(m)
        mid = bf(res)
        lo_ = bf(res - mid)
        assert np.abs(res[:, :, [k for k in range(128) if not (cs.start <= k < cs.stop)]]).max() == 0.0
        out[:, 0, :, v, :] = mid[:, :, cs]
        out[:, 1, :, v, :] = lo_[:, :, cs]
    return out.astype(ml_dtypes.bfloat16)


DEV_STAGE = 2
DEV_NBLK = None
DEV_CUT = 99
DEV_NOCONVPRE = False


class _Cut(Exception):
    pass


def build_program(seq_p, seq_s, debug_taps=False, _order=None):
    if _order is None:
        _order = build_program(seq_p, seq_s, debug_taps, _order="collect")
    collect = (_order == "collect")
    assert seq_p % T == 0 and seq_s % T == 0 and seq_p >= 2 * T and seq_s >= 2 * T
    nc = bass.Bass("TRN2", target_bir_lowering=False)
    seqs = [seq_p, seq_s]

    def din(name, shape, dt=F32):
        return nc.dram_tensor(name, list(shape), dt, kind="ExternalInput").ap()

    def dint(name, shape, dt):
        return nc.dram_tensor(name, list(shape), dt, kind="Internal").ap()

    x_in = [din("x_p", [seq_p, D]), din("x_s", [seq_s, D])]
    c_in = din("c", [2, D])
    shapes = {
        "l0_ada_w": [D, 6 * D], "l0_ada_b": [6 * D], "l0_in_w": [D, 1536], "l0_pool_w": [4, 128, 128],
        "l0_pool_scale": [512], "l0_sgu_ln_g": [512], "l0_sgu_ln_b": [512], "l0_sgu_w": [4, 128, 128],
        "l0_sgu_b": [4, 128], "l0_out_w": [D, D], "l0_ln1_g": [D], "l0_ln1_b": [D],
        "l0_mlp_w1": [D, DFF], "l0_mlp_w2": [DFF, D], "l0_ln2_g": [D], "l0_ln2_b": [D],
        "l1_ada_w": [D, 6 * D], "l1_ada_b": [6 * D], "l1_pw1_w": [D, 2 * D], "l1_pw1_b": [2 * D],
        "l1_dw_w": [TAPS, D], "l1_dw_b": [D], "l1_cnorm_g": [D], "l1_cnorm_b": [D],
        "l1_pw2_w": [D, D], "l1_pw2_b": [D], "l1_ln1_g": [D], "l1_ln1_b": [D],
        "l1_mlp_w1": [D, DFF], "l1_mlp_w2": [DFF, D], "l1_ln2_g": [D], "l1_ln2_b": [D],
    }
    W = {n: din(n, shapes[n]) for n in WEIGHT_NAMES}
    pm_in = [din("pmat_p", [5, 128, 4, 128], BF16), din("pmat_s", [5, 128, 4, 128], BF16)]
    pmx_in = din("pmat_x", [128, 2, 4, 3, 8], BF16)
    y_out = [nc.dram_tensor("y_p", [seq_p, D], F32, kind="ExternalOutput").ap(),
             nc.dram_tensor("y_s", [seq_s, D], F32, kind="ExternalOutput").ap()]
    xmid = [dint("xmid_p", [128, KC, seq_p], F32), dint("xmid_s", [128, KC, seq_s], F32)]
    taps = {}

    with contextlib.ExitStack() as st:
        S = Sched(nc, st)

        def sb(name, shape, dt):
            return st.enter_context(nc.sbuf_tensor(name, list(shape), dt))

        banks = [st.enter_context(nc.psum_tensor("bank%d" % i, [128, 512], F32)) for i in range(8)]
        bstate = {"n": 0}

        def next_bank():
            i = bstate["n"] % 8
            bstate["n"] += 1
            return banks[i], ("bank", i)

        pieces = {}
        cast_jobs = []

        def add_piece(pname, srcs, kc, cols):
            ap = dint("wp_" + pname, [128, kc, cols], BF16)
            pieces[pname] = (ap, kc, cols)
            cast_jobs.append((pname, srcs))

        def wsrc(wname, c0, c1):
            return W[wname].rearrange("(k p) c -> p k c", p=128)[:, :, c0:c1]

        for L in range(2):
            for j in range(12):
                add_piece("ada%d_%d" % (L, j), [(0, 512, wsrc("l%d_ada_w" % L, j * 512, (j + 1) * 512))], 8, 512)
        for j in range(3):
            add_piece("in_%d" % j, [(0, 512, wsrc("l0_in_w", j * 512, (j + 1) * 512))], 8, 512)
        for j in range(2):
            add_piece("out_%d" % j, [(0, 512, wsrc("l0_out_w", j * 512, (j + 1) * 512))], 8, 512)
        for L in range(2):
            for j in range(8):
                add_piece("w1_%d_%d" % (L, j), [(0, 512, wsrc("l%d_mlp_w1" % L, j * 512, (j + 1) * 512))], 8, 512)
            for j in range(8):
                add_piece("w2_%d_%d" % (L, j), [(0, 128, wsrc("l%d_mlp_w2" % L, j * 128, (j + 1) * 128))], 32, 128)
        for j in range(4):
            add_piece("pw1_%d" % j, [(0, 256, wsrc("l1_pw1_w", j * 256, (j + 1) * 256)),
                                     (256, 512, wsrc("l1_pw1_w", D + j * 256, D + (j + 1) * 256))], 8, 512)
        for j in range(2):
            add_piece("pw2_%d" % j, [(0, 512, wsrc("l1_pw2_w", j * 512, (j + 1) * 512))], 8, 512)

        def issue_casts(names):
            for pname in names:
                srcs = dict(cast_jobs)[pname]
                ap = pieces[pname][0]
                for (c0, c1, src) in srcs:
                    S.dma_once("pool", ap[:, :, c0:c1], src, st, reads=[], writes=[("piece", pname, c0)])

        wslots = [sb("wslot%d" % i, [128, 4096], BF16) for i in range(N_WSLOTS)]
        wstate = {"order": [], "issued": 0, "got": 0}

        def piece_regions(pname):
            return [("piece", pname, c0) for (c0, _, _) in dict(cast_jobs)[pname]]

        def w_issue_upto(n):
            while wstate["issued"] < min(n, len(wstate["order"])):
                i = wstate["issued"]
                pname = wstate["order"][i]
                ap, kc, cols = pieces[pname]
                slot = wslots[i % N_WSLOTS]
                S.dma("sp", slot[:, 0:kc * cols].rearrange("p (k c) -> p k c", k=kc), ap,
                      reads=piece_regions(pname), writes=[("wslot", i % N_WSLOTS)])
                wstate["issued"] += 1

        def w_get(pname):
            i = wstate["got"]
            if collect:
                wstate["order"].append(pname)
            assert wstate["order"][i] == pname, (wstate["order"][i], pname)
            w_issue_upto(i + N_WSLOTS - 1)
            wstate["got"] += 1
            ap, kc, cols = pieces[pname]
            slot = wslots[i % N_WSLOTS]
            return slot[:, 0:kc * cols].rearrange("p (k c) -> p k c", k=kc), ("wslot", i % N_WSLOTS)

        order = []
        for L in range(2):
            order += ["ada%d_%d" % (L, j) for j in range(12)]
        l0_order = ["in_0", "in_2", "in_1", "out_0", "out_1"] + ["w1_0_%d" % j for j in range(8)] + \
                   ["w2_0_%d" % j for j in range(8)]
        l1_order = ["pw1_%d" % j for j in range(4)] + ["pw2_0", "pw2_1"] + ["w1_1_%d" % j for j in range(8)] + \
                   ["w2_1_%d" % j for j in range(8)]
        blocks = []
        for g in range(2):
            nb = seqs[g] // T
            for b in range(nb):
                blocks.append((g, b * T, b == 0, b == nb - 1))
        for _ in blocks:
            order += l0_order
        for _ in blocks:
            order += l1_order
        wstate["order"] = [] if collect else list(_order)

        ident = sb("ident", [128, 128], F32)
        ident_b = sb("ident_b", [128, 128], BF16)
        ones_b = sb("ones_b", [128, 128], BF16)
        mean_b = sb("mean_b", [128, 128], BF16)
        eps_t = sb("eps_t", [128, 2], F32)
        NPAR = 1024
        PR = sb("PR", [128, NPAR], F32)
        prcol = {"n": 0}
        prmap = {}

        def pr_alloc(name, n):
            o = prcol["n"]
            prcol["n"] += n
            assert prcol["n"] <= NPAR
            prmap[name] = (o, n)
            return PR[:, o:o + n]

        def pr(name):
            o, n = prmap[name]
            return PR[:, o:o + n]

        def prk(name, k):
            o, n = prmap[name]
            return PR[:, o + k:o + k + 1]

        mod = [sb("mod%d" % L, [128, 48, 2], F32) for L in range(2)]
        cT = sb("cT", [128, KC, 2], F32)
        sT_c = sb("sT_c", [128, KC, 2], BF16)

        xTs = [sb("xT%d" % i, [128, KC, 544], F32) for i in range(2)]
        bufA = sb("bufA", [128, KC, T], F32)
        hT = sb("hT", [128, KC, 544], BF16)
        h1T = sb("h1T", [128, KC, T], BF16)
        vb = sb("vb", [128, KC, T], BF16)
        v2b = sb("v2b", [128, KC, T], BF16)
        stats = sb("stats", [128, 4, T], F32)
        hidT = sb("hidT", [128, 32, T], BF16)
        hid_f32 = hidT[:].rearrange("p a b -> p (a b)").bitcast(F32)
        relu_t = [sb("relu%d" % i, [128, T], BF16) for i in range(3)]
        SCR_F32 = 9600
        scr = sb("scr", [128, SCR_F32], F32)
        pm_sb = sb("pm_sb", [128, 5, 512], BF16)
        pm_x = sb("pm_x", [128, 2 * 4 * 3 * 8], BF16)
        sguwT = sb("sguwT", [128, 4, 128], BF16)
        C_hi = sb("C_hi", [128, 4, 128], BF16)
        C_lo = sb("C_lo", [128, 4, 128], BF16)
        poolw_b = sb("poolw_b", [128, 4, 128], BF16)
        g_bc = sb("g_bc", [128, 512], F32)

        def carve(off_f32, shape, dt):
            n = int(np.prod(shape[1:]))
            nf = n if dt == F32 else (n + 1) // 2
            v = scr[:, off_f32:off_f32 + nf]
            if dt != F32:
                v = v.bitcast(dt)
            if len(shape) == 3:
                v = v.rearrange("p (a b) -> p a b", a=shape[1])
            return v, off_f32 + nf

        o = 0
        a_tok, o = carve(o, [128, 5, 512], BF16)
        uT, o = carve(o, [128, 4, T], F32)
        v_tok, o = carve(o, [128, 4, 512], F32)
        nG, o = carve(o, [128, 4, 512], BF16)
        pooledT, o = carve(o, [128, 4, T], BF16)
        yabT, o = carve(o, [128, 8, T], BF16)
        assert o <= SCR_F32, o
        o = 0
        gT, o = carve(o, [128, KC, 544], BF16)
        sig, o = carve(o, [128, 2, 544], F32)
        sT, o = carve(o, [128, KC, T], BF16)
        dg, o = carve(o, [128, 2, TAPS * 128], BF16)
        assert o <= SCR_F32, o

        small = sb("small", [128, 64], F32)

        def tap(name, ap, shape, reads):
            if not debug_taps:
                return
            t = nc.dram_tensor("tap_" + name, list(shape), ap.dtype, kind="ExternalOutput").ap()
            taps[name] = t
            S.dma("sp", t, ap, reads=reads)

        S.op("pool", lambda e: e.memset(ident[:], 0.0), writes=["ident"])
        S.op("pool", lambda e: e.affine_select(out=ident[:], in_=ident[:], pattern=[[-1, 128]],
                                                compare_op=ALU.not_equal, fill=1.0, base=0,
                                                channel_multiplier=1), reads=["ident"], writes=["ident"])
        S.op("pool", lambda e: e.tensor_copy(out=ident_b[:], in_=ident[:]), reads=["ident"], writes=["ident_b"])
        S.op("pool", lambda e: e.memset(ones_b[:], 1.0), writes=["ones_b"])
        S.op("pool", lambda e: e.memset(mean_b[:], 1.0 / D), writes=["mean_b"])
        S.op("pool", lambda e: e.memset(eps_t[:, 0:1], LN_EPS), writes=["eps_t"])
        S.op("pool", lambda e: e.memset(eps_t[:, 1:2], EPS_POST), reads=["eps_t"], writes=["eps_t"])

        issue_casts(order[:12])
        issue_casts(l0_order)
        issue_casts(order[12:24])
        issue_casts(l1_order)

        VS = sb("VS", [128, 4, 128], F32)
        S.op("dve", lambda e: e.memset(VS[:].rearrange("p a b -> p (a b)"), 0.0), writes=["VS"])
        vrow = {"n": 0}

        def load_vec(name, src_ap, n):
            if (vrow["n"] % 128) + n > 128:
                vrow["n"] = (vrow["n"] // 128 + 1) * 128
            r = vrow["n"]
            vrow["n"] += n
            assert vrow["n"] <= 512
            prmap[name] = (r, n)
            S.dma("sp", VS[r % 128:r % 128 + n, r // 128, :], src_ap.rearrange("(k p) -> k p", p=128),
                  reads=[], writes=["VS"])

        for L in range(2):
            load_vec("ada_b%d" % L, W["l%d_ada_b" % L], 48)
        for L in range(2):
            for nm in ["ln1_g", "ln1_b", "ln2_g", "ln2_b"]:
                load_vec("%s%d" % (nm, L), W["l%d_%s" % (L, nm)], 8)
        load_vec("pool_scale", W["l0_pool_scale"], 4)
        load_vec("sgu_ln_b", W["l0_sgu_ln_b"], 4)
        load_vec("pw1_b", W["l1_pw1_b"], 16)
        load_vec("dw_b", W["l1_dw_b"], 8)
        load_vec("cn_g", W["l1_cnorm_g"], 8)
        load_vec("cn_b", W["l1_cnorm_b"], 8)
        load_vec("pw2_b", W["l1_pw2_b"], 8)
        load_vec("c0", c_in[0], 8)
        load_vec("c1", c_in[1], 8)
        vrow["n"] = 256
        load_vec("dww_a", W["l1_dw_w"][0:16, :].rearrange("t f -> (t f)"), 128)
        load_vec("dww_b", W["l1_dw_w"][16:31, :].rearrange("t f -> (t f)"), 120)
        prcol["n"] = 512
        for i in range(4):
            bkv, bkvr = next_bank()
            S.op("pe", [lambda e, i=i, bkv=bkv: e.transpose(bkv[:, 0:128], VS[:, i, :], ident[:])],
                 reads=["VS", "ident"], writes=[bkvr])
            S.op("dve", lambda e, i=i, bkv=bkv: e.tensor_copy(out=PR[:, i * 128:(i + 1) * 128], in_=bkv[:, 0:128]),
                 reads=[bkvr], writes=["PRraw"])
        wc = PR[:, 256:256 + TAPS * 8].rearrange("p (t k) -> p k t", k=8)
        for g in range(2):
            S.op("dve", lambda e, g=g: e.tensor_copy(out=cT[:, :, g], in_=pr("c%d" % g)), reads=["PRraw"], writes=["cT"])
        S.dma("sp", g_bc[:], W["l0_sgu_ln_g"].partition_broadcast(128), writes=["g_bc"])
        S.dma("sp", pm_sb[:], pm_in[0].rearrange("v p g c -> p v (g c)"), writes=["pm_sb"])
        S.dma("sp", pm_x[:], pmx_in.rearrange("p a g v c -> p (a g v c)"), writes=["pm_sb"])
        pw_f = hid_f32[:, 0:512].rearrange("p (g e) -> p g e", g=4)
        S.dma("sp", pw_f, W["l0_pool_w"].rearrange("g d e -> d g e"), writes=[("hid", 0)])
        S.op("dve", lambda e: e.tensor_copy(out=poolw_b[:], in_=pw_f), reads=[("hid", 0)], writes=["poolw_b"])
        sw_f = hid_f32[:, 512:1024].rearrange("p (h q) -> p h q", h=4)
        S.dma("sp", sw_f, W["l0_sgu_w"].rearrange("h p q -> p h q"), writes=[("hid", 0)])
        bk, bkr = next_bank()
        S.op("pe", [(lambda e, h=h: e.transpose(bk[:, h * 128:(h + 1) * 128], sw_f[:, h, :], ident[:]))
                    for h in range(4)], reads=[("hid", 0), "ident"], writes=[bkr])
        S.op("dve", lambda e: e.tensor_copy(out=sguwT[:].rearrange("p h q -> p (h q)"), in_=bk[:]),
             reads=[bkr], writes=["sguwT"])
        bk2, bk2r = next_bank()
        S.op("pe", [lambda e: e.matmul(bk2[:], lhsT=ones_b[:], rhs=sguwT[:].rearrange("p h q -> p (h q)"),
                                       start=True, stop=True)], reads=["ones_b", "sguwT"], writes=[bk2r])
        sgub_bc = hid_f32[:, 1024:1536]
        S.dma("sp", sgub_bc, W["l0_sgu_b"].rearrange("h p -> (h p)").partition_broadcast(128), writes=[("hid", 0)])
        Cf = hid_f32[:, 1536:2048]
        Ct = hid_f32[:, 2048:2560]
        for h in range(4):
            S.op("dve", lambda e, h=h: e.scalar_tensor_tensor(
                out=Cf[:, h * 128:(h + 1) * 128], in0=bk2[:, h * 128:(h + 1) * 128],
                scalar=prk("sgu_ln_b", h), in1=sgub_bc[:, h * 128:(h + 1) * 128],
                op0=ALU.mult, op1=ALU.add),
                reads=[bk2r, ("hid", 0), "PRraw"], writes=[("hid", 0)])
        S.op("dve", lambda e: e.tensor_copy(out=C_hi[:].rearrange("p h q -> p (h q)"), in_=Cf), reads=[("hid", 0)], writes=["C_hi"])
        S.op("dve", lambda e: e.tensor_tensor(out=Ct, in0=Cf, in1=C_hi[:].rearrange("p h q -> p (h q)"), op=ALU.subtract),
             reads=[("hid", 0), "C_hi"], writes=[("hid", 0)])
        S.op("dve", lambda e: e.tensor_copy(out=C_lo[:].rearrange("p h q -> p (h q)"), in_=Ct), reads=[("hid", 0)], writes=["C_lo"])
        C_l3 = VS[:].rearrange("p a b -> p (a b)").bitcast(BF16)[:, 0:512]
        S.op("dve", lambda e: e.tensor_tensor(out=Cf, in0=Ct, in1=C_lo[:].rearrange("p h q -> p (h q)"), op=ALU.subtract),
             reads=[("hid", 0), "C_lo"], writes=[("hid", 0)])
        S.op("dve", lambda e: e.tensor_copy(out=C_l3, in_=Cf), reads=[("hid", 0), "VS"], writes=["VS"])

        S.op("act", lambda e: e.activation(out=sT_c[:].rearrange("p k g -> p (k g)"),
                                           in_=cT[:].rearrange("p k g -> p (k g)"), func=AF.Silu),
             reads=["cT"], writes=["sT_c"])
        def do_ada(L):
            bkm, bkmr = next_bank()
            for j in range(12):
                wv, wr = w_get("ada%d_%d" % (L, j))
                fns = []
                for mm in range(4):
                    m = j * 4 + mm
                    for k in range(KC):
                        fns.append(lambda e, m=m, mm=mm, k=k, wv=wv: e.matmul(
                            bkm[:, m * 2:m * 2 + 2], lhsT=wv[:, k, mm * 128:(mm + 1) * 128], rhs=sT_c[:, k, :],
                            start=(k == 0), stop=(k == KC - 1)))
                S.op("pe", fns, reads=[wr, "sT_c"], writes=[bkmr])
                if debug_taps and L == 0 and j in (0, 11):
                    tap("slot%d" % j, wv, [128, 8, 512], [wr])
            if debug_taps and L == 0:
                S.op("dve", lambda e, bkm=bkm: e.tensor_copy(out=stats[:, 0, 0:96], in_=bkm[:, 0:96]), reads=[bkmr], writes=[("stats", 0)])
                tap("praw", stats[:, 0, 0:96], [128, 96], [("stats", 0)])
            S.op("dve", lambda e, L=L, bkm=bkm: e.tensor_tensor(
                out=mod[L][:], in0=bkm[:, 0:96].rearrange("p (m g) -> p m g", g=2),
                in1=pr("ada_b%d" % L).unsqueeze(2).to_broadcast([128, 48, 2]), op=ALU.add),
                reads=[bkmr, "PRraw"], writes=[("mod", L)])

        def modv(L, which, g):
            return mod[L][:, which * 8:(which + 1) * 8, g]

        def do_derived(L):
            for g in range(2):
                sfx = "%d%d" % (L, g)
                rd = [("mod", L)]
                for nm, which in [("A_m", 1), ("A_f", 4)]:
                    dst = pr_alloc(nm + sfx, 8)
                    S.op("dve", lambda e, dst=dst, L=L, which=which, g=g: e.tensor_scalar(
                        out=dst, in0=modv(L, which, g), scalar1=1.0, scalar2=None, op0=ALU.add),
                        reads=rd, writes=[("pr", nm + sfx)])
                for nm, which in [("B_m", 0), ("B_f", 3)]:
                    dst = pr_alloc(nm + sfx, 8)
                    S.op("dve", lambda e, dst=dst, L=L, which=which, g=g: e.tensor_copy(
                        out=dst, in_=modv(L, which, g)), reads=rd, writes=[("pr", nm + sfx)])
                for nm, which in [("G_m", 2), ("G_f", 5)]:
                    dst = pr_alloc(nm + sfx, 8)
                    S.op("dve", lambda e, dst=dst, L=L, which=which, g=g: e.tensor_scalar(
                        out=dst, in0=modv(L, which, g), scalar1=1.0 / ALPHA, scalar2=None, op0=ALU.mult),
                        reads=rd, writes=[("pr", nm + sfx)])
                dst = pr_alloc("Gp" + sfx, 8)
                S.op("dve", lambda e, dst=dst, L=L, sfx=sfx: e.tensor_tensor(
                    out=dst, in0=pr("ln1_g%d" % L), in1=pr("A_f" + sfx), op=ALU.mult),
                    reads=["PRraw", ("pr", "A_f" + sfx)], writes=[("pr", "Gp" + sfx)])
                dst = pr_alloc("Bp" + sfx, 8)
                S.op("dve", lambda e, dst=dst, L=L, sfx=sfx: e.tensor_tensor(
                    out=dst, in0=pr("ln1_b%d" % L), in1=pr("A_f" + sfx), op=ALU.mult),
                    reads=["PRraw", ("pr", "A_f" + sfx)], writes=[("pr", "Bp" + sfx)])
                S.op("dve", lambda e, dst=dst, sfx=sfx: e.tensor_tensor(
                    out=dst, in0=dst, in1=pr("B_f" + sfx), op=ALU.add),
                    reads=[("pr", "Bp" + sfx), ("pr", "B_f" + sfx)], writes=[("pr", "Bp" + sfx)])
                if L == 1:
                    dst = pr_alloc("bg" + sfx, 8)
                    S.op("dve", lambda e, dst=dst, sfx=sfx: e.tensor_tensor(
                        out=dst, in0=pr("pw2_b"), in1=pr("G_m" + sfx), op=ALU.mult),
                        reads=["PRraw", ("pr", "G_m" + sfx)], writes=[("pr", "bg" + sfx)])

        do_ada(0)
        do_derived(0)

        def layer_norm_fm(vreg, eps_col):
            bm, bmr = next_bank()
            bq, bqr = next_bank()
            for k in range(KC):
                S.op("act", lambda e, k=k: e.activation(out=vb[:, k, :], in_=bufA[:, k, :], func=AF.Copy),
                     reads=[vreg(k)], writes=[("vb", k)])
                S.op("act", lambda e, k=k: e.activation(out=v2b[:, k, :], in_=bufA[:, k, :], func=AF.Square),
                     reads=[vreg(k)], writes=[("v2b", k)])
            for k in range(KC):
                S.op("pe", [lambda e, k=k: e.matmul(bm[:], lhsT=mean_b[:], rhs=vb[:, k, :],
                                                    start=(k == 0), stop=(k == KC - 1))],
                     reads=[("vb", k), "mean_b"], writes=[bmr])
            for k in range(KC):
                S.op("pe", [lambda e, k=k: e.matmul(bq[:], lhsT=mean_b[:], rhs=v2b[:, k, :],
                                                    start=(k == 0), stop=(k == KC - 1))],
                     reads=[("v2b", k), "mean_b"], writes=[bqr])
            yield
            S.op("act", lambda e: e.activation(out=stats[:, 0, :], in_=bm[:], func=AF.Square),
                 reads=[bmr], writes=[("stats", 0)])
            S.op("dve", lambda e: e.tensor_tensor(out=stats[:, 1, :], in0=bq[:], in1=stats[:, 0, :], op=ALU.subtract),
                 reads=[bqr, ("stats", 0)], writes=[("stats", 1)])
            S.op("dve", lambda e: e.tensor_scalar(out=stats[:, 1, :], in0=stats[:, 1, :], scalar1=0.0, scalar2=None,
                                                  op0=ALU.max), reads=[("stats", 1)], writes=[("stats", 1)])
            S.op("act", lambda e: e.activation(out=stats[:, 1, :], in_=stats[:, 1, :], func=AF.Sqrt,
                                               bias=eps_t[:, eps_col:eps_col + 1], scale=1.0),
                 reads=[("stats", 1), "eps_t"], writes=[("stats", 1)])
            S.op("dve", lambda e: e.reciprocal(out=stats[:, 2, :], in_=stats[:, 1, :]),
                 reads=[("stats", 1)], writes=[("stats", 2)])
            S.op("dve", lambda e: e.tensor_tensor(out=stats[:, 3, :], in0=bm[:], in1=stats[:, 2, :], op=ALU.mult),
                 reads=[bmr, ("stats", 2)], writes=[("stats", 3)])
            yield
            for k in range(KC):
                S.op("dve", lambda e, k=k: e.tensor_tensor(out=bufA[:, k, :], in0=bufA[:, k, :], in1=stats[:, 2, :],
                                                           op=ALU.mult),
                     reads=[vreg(k), ("stats", 2)], writes=[vreg(k)])
                S.op("dve", lambda e, k=k: e.tensor_tensor(out=bufA[:, k, :], in0=bufA[:, k, :], in1=stats[:, 3, :],
                                                           op=ALU.subtract),
                     reads=[vreg(k), ("stats", 3)], writes=[vreg(k)])
                if k % 2 == 1:
                    yield

        def run(gen):
            for _ in gen:
                pass

        def interleave(ga, gb):
            da = db = False
            while not (da and db):
                if not da:
                    try:
                        next(ga)
                    except StopIteration:
                        da = True
                if not db:
                    try:
                        next(gb)
                    except StopIteration:
                        db = True

        def mlp(L, sfx, xres):
            for j in range(8):
                wv, wr = w_get("w1_%d_%d" % (L, j))
                for mm in range(4):
                    m = j * 4 + mm
                    bk, bkr = next_bank()
                    S.op("pe", [(lambda e, k=k, mm=mm, wv=wv, bk=bk: e.matmul(
                        bk[:], lhsT=wv[:, k, mm * 128:(mm + 1) * 128], rhs=h1T[:, k, :],
                        start=(k == 0), stop=(k == KC - 1))) for k in range(KC)],
                        reads=[wr] + [("h1T", k) for k in range(KC)], writes=[bkr])
                    rt = relu_t[m % 3]
                    S.op("act", lambda e, rt=rt, bk=bk: e.activation(out=rt[:], in_=bk[:], func=AF.Relu),
                         reads=[bkr], writes=[("relu", m % 3)])
                    S.op("dve", lambda e, rt=rt, m=m: e.tensor_tensor(out=hidT[:, m, :], in0=rt[:], in1=rt[:],
                                                                     op=ALU.mult),
                         reads=[("relu", m % 3)], writes=[("hid", m)])
            for m in range(KC):
                wv, wr = w_get("w2_%d_%d" % (L, m))
                bk, bkr = next_bank()
                S.op("pe", [(lambda e, k=k, wv=wv, bk=bk: e.matmul(
                    bk[:], lhsT=wv[:, k, :], rhs=hidT[:, k, :], start=(k == 0), stop=(k == 31)))
                    for k in range(32)],
                    reads=[wr] + [("hid", k) for k in range(32)], writes=[bkr])
                xa, xr = xres(m)
                S.op("dve", lambda e, m=m, bk=bk, xa=xa: e.scalar_tensor_tensor(
                    out=bufA[:, m, :], in0=bk[:], scalar=prk("G_f" + sfx, m), in1=xa,
                    op0=ALU.mult, op1=ALU.add),
                    reads=[bkr, xr, ("pr", "G_f" + sfx)], writes=[("bufA", m)])

        def preg(name):
            return "PRraw" if prmap[name][0] < 512 else ("pr", name)

        def affine_from_bufA(eng, out_fn, out_reg_fn, gname, bname):
            for m in range(KC):
                oa = out_fn(m)
                if eng == "act":
                    S.op("act", lambda e, m=m, oa=oa: e.activation(
                        out=oa, in_=bufA[:, m, :], func=AF.Identity, scale=prk(gname, m), bias=prk(bname, m)),
                        reads=[("bufA", m), preg(gname), preg(bname)], writes=[out_reg_fn(m)])
                else:
                    S.op(eng, lambda e, m=m, oa=oa: e.tensor_scalar(
                        out=oa, in0=bufA[:, m, :], scalar1=prk(gname, m), scalar2=prk(bname, m),
                        op0=ALU.mult, op1=ALU.add),
                        reads=[("bufA", m), preg(gname), preg(bname)], writes=[out_reg_fn(m)])

        x_tok = scr[:, 0:5 * 1024].rearrange("p (j f) -> p j f", j=5)
        XREG = [("a_tok", j) for j in range(5)] + [("uT", m) for m in range(4)] + [("v_tok", j) for j in range(4)]

        if debug_taps:
            tap('PR', PR[:], [128, NPAR], ['PRraw'] + [('pr', n) for n in prmap if prmap[n][0] >= 512])
            tap('mod0', mod[0][:], [128, 48, 2], [('mod', 0)])
            tap('sTc', sT_c[:], [128, KC, 2], ['sT_c'])
            tap('wp0', pieces['ada0_0'][0], [128, 8, 512], piece_regions('ada0_0'))
            tap('wp1', pieces['ada1_11'][0], [128, 8, 512], piece_regions('ada1_11'))
            tap('Chi', C_hi[:], [128, 4, 128], ['C_hi'])
            tap('Clo', C_lo[:], [128, 4, 128], ['C_lo'])
            tap('sguwT', sguwT[:], [128, 4, 128], ['sguwT'])
        def l0_block(bi, g, s, first, last):
            sfx = "0%d" % g
            xT = xTs[bi % 2]
            XT = "xT%d" % (bi % 2)
            Sg = seqs[g]
            if first:
                S.op("dve", lambda e: e.memset(x_tok[0:8, 0, :], 0.0), writes=XREG)
            if last:
                S.op("dve", lambda e: e.memset(x_tok[0:24, 4, :], 0.0), writes=XREG)
            for j in range(5):
                t0 = s - HALO0 + 128 * j
                rows = 128 if j < 4 else 24
                r0 = max(0, -t0)
                r1 = min(rows, Sg - t0)
                S.dma("sp", x_tok[r0:r1, j, :], x_in[g][t0 + r0:t0 + r1, :], writes=XREG)
            yield
            bx, bxr = next_bank()
            for k in range(KC):
                bk, bkr = next_bank()
                S.op("pe", [(lambda e, j=j, k=k, bk=bk: e.transpose(
                    bk[:, j * 128:(j + 1) * 128], x_tok[:, j, k * 128:(k + 1) * 128], ident[:])) for j in range(4)],
                    reads=XREG + ["ident"], writes=[bkr])
                S.op("act", lambda e, k=k, bk=bk: e.activation(out=xT[:, k, 0:512], in_=bk[:], func=AF.Copy),
                     reads=[bkr], writes=[(XT, k)])
                S.op("dve", lambda e, k=k, bk=bk: e.tensor_scalar(
                    out=hT[:, k, 0:512], in0=bk[:], scalar1=prk("A_m" + sfx, k), scalar2=prk("B_m" + sfx, k),
                    op0=ALU.mult, op1=ALU.add),
                    reads=[bkr, ("pr", "A_m" + sfx), ("pr", "B_m" + sfx)], writes=[("hT", k)])
                yield
            S.op("pe", [(lambda e, k=k: e.transpose(bx[:, k * 24:(k + 1) * 24], x_tok[0:24, 4, k * 128:(k + 1) * 128],
                                                    ident[0:24, 0:24])) for k in range(KC)],
                 reads=XREG + ["ident"], writes=[bxr])
            S.op("act", lambda e: e.activation(out=xT[:, :, 512:536], in_=bx[:, 0:192].rearrange("p (k c) -> p k c", k=KC),
                                               func=AF.Copy), reads=[bxr], writes=[(XT, k) for k in range(KC)])
            for k in range(KC):
                S.op("dve", lambda e, k=k: e.tensor_scalar(
                    out=hT[:, k, 512:536], in0=bx[:, k * 24:(k + 1) * 24], scalar1=prk("A_m" + sfx, k),
                    scalar2=prk("B_m" + sfx, k), op0=ALU.mult, op1=ALU.add),
                    reads=[bxr, ("pr", "A_m" + sfx), ("pr", "B_m" + sfx)], writes=[("hT", k)])
            HREG = [("hT", k) for k in range(KC)]
            if debug_taps and bi == 0:
                tap("h0T", hT[:, :, 0:536], [128, KC, 536], HREG)

            yield
            wv, wr = w_get("in_0")
            for j in range(5):
                rows = 128 if j < 4 else 24
                bk, bkr = next_bank()
                S.op("pe", [(lambda e, k=k, j=j, rows=rows, bk=bk, wv=wv: e.matmul(
                    bk[0:rows, :], lhsT=hT[:, k, 128 * j:128 * j + rows], rhs=wv[:, k, :],
                    start=(k == 0), stop=(k == KC - 1))) for k in range(KC)],
                    reads=[wr] + HREG, writes=[bkr])
                S.op("act" if j % 2 == 0 else "dve",
                     (lambda e, j=j, rows=rows, bk=bk: e.activation(out=a_tok[0:rows, j, :], in_=bk[0:rows, :], func=AF.Copy))
                     if j % 2 == 0 else
                     (lambda e, j=j, rows=rows, bk=bk: e.tensor_copy(out=a_tok[0:rows, j, :], in_=bk[0:rows, :])),
                     reads=[bkr], writes=[("a_tok", j)])
                yield
            wv, wr = w_get("in_2")
            for j in range(4):
                bk, bkr = next_bank()
                c0 = HALO0 + 128 * j
                S.op("pe", [(lambda e, k=k, c0=c0, bk=bk, wv=wv: e.matmul(
                    bk[:], lhsT=hT[:, k, c0:c0 + 128], rhs=wv[:, k, :],
                    start=(k == 0), stop=(k == KC - 1))) for k in range(KC)],
                    reads=[wr] + HREG, writes=[bkr])
                S.op("act", lambda e, j=j, bk=bk: e.activation(out=v_tok[:, j, :], in_=bk[:], func=AF.Gelu),
                     reads=[bkr], writes=[("v_tok", j)])
                so = j * 16
                S.op("dve", lambda e, j=j, so=so: e.bn_stats(out=small[:, so:so + 6], in_=v_tok[:, j, :]),
                     reads=[("v_tok", j)], writes=[("small", j)])
                S.op("dve", lambda e, so=so: e.bn_aggr(out=small[:, so + 6:so + 8], in_=small[:, so:so + 6]),
                     reads=[("small", j)], writes=[("small", j)])
                S.op("act", lambda e, so=so: e.activation(out=small[:, so + 8:so + 9], in_=small[:, so + 7:so + 8],
                                                          func=AF.Sqrt, bias=eps_t[:, 0:1], scale=1.0),
                     reads=[("small", j), "eps_t"], writes=[("small", j)])
                S.op("dve", lambda e, so=so: e.reciprocal(out=small[:, so + 9:so + 10], in_=small[:, so + 8:so + 9]),
                     reads=[("small", j)], writes=[("small", j)])
                S.op("dve", lambda e, so=so: e.tensor_scalar(
                    out=small[:, so + 10:so + 11], in0=small[:, so + 6:so + 7], scalar1=small[:, so + 9:so + 10],
                    scalar2=-1.0, op0=ALU.mult, op1=ALU.mult), reads=[("small", j)], writes=[("small", j)])
                S.op("act", lambda e, j=j, so=so: e.activation(
                    out=v_tok[:, j, :], in_=v_tok[:, j, :], func=AF.Identity,
                    scale=small[:, so + 9:so + 10], bias=small[:, so + 10:so + 11]),
                    reads=[("v_tok", j), ("small", j)], writes=[("v_tok", j)])
                S.op("dve", lambda e, j=j: e.tensor_tensor(out=nG[:, j, :], in0=v_tok[:, j, :], in1=g_bc[:], op=ALU.mult),
                     reads=[("v_tok", j), "g_bc"], writes=[("nG", j)])
                yield
            yield "F1"
            wv, wr = w_get("in_1")
            for m in range(4):
                bk, bkr = next_bank()
                S.op("pe", [(lambda e, k=k, m=m, bk=bk, wv=wv: e.matmul(
                    bk[:], lhsT=wv[:, k, m * 128:(m + 1) * 128], rhs=hT[:, k, HALO0:HALO0 + T],
                    start=(k == 0), stop=(k == KC - 1))) for k in range(KC)],
                    reads=[wr] + HREG, writes=[bkr])
                S.op("act", lambda e, m=m, bk=bk: e.activation(out=uT[:, m, :], in_=bk[:], func=AF.Gelu),
                     reads=[bkr], writes=[("uT", m)])
                yield
            for gg in range(4):
                bk, bkr = next_bank()
                fns = []
                for j in range(4):
                    vm = 1 if (first and j == 0) else (2 if (last and j == 3) else 0)
                    vn = 4 if (last and j == 3) else 3
                    fns.append(lambda e, j=j, gg=gg, vm=vm, bk=bk: e.matmul(
                        bk[:, j * 128:(j + 1) * 128], lhsT=a_tok[:, j, gg * 128:(gg + 1) * 128],
                        rhs=pm_sb[:, vm, gg * 128:(gg + 1) * 128], start=True, stop=False))
                    bnd = (first and j == 0) or (last and j == 3)
                    fns.append(lambda e, j=j, gg=gg, vn=vn, bk=bk, bnd=bnd: e.matmul(
                        bk[:, j * 128:(j + 1) * 128], lhsT=a_tok[0:16, j + 1, gg * 128:(gg + 1) * 128],
                        rhs=pm_sb[0:16, vn, gg * 128:(gg + 1) * 128], start=False, stop=(not bnd)))
                    if bnd:
                        pmx = pm_x[:].rearrange("p (a g v c) -> p a g v c", a=2, g=4, v=3)
                        ext = []
                        for term in range(2):
                            if first and j == 0:
                                ext.append((bk[:, 0:8], a_tok[0:32, 0, gg * 128:(gg + 1) * 128], pmx[0:32, term, gg, 0, :]))
                            else:
                                c0_ = 3 * 128 + 120
                                ext.append((bk[:, c0_:c0_ + 8], a_tok[64:128, 3, gg * 128:(gg + 1) * 128],
                                            pmx[64:128, term, gg, 1, :]))
                                ext.append((bk[:, c0_:c0_ + 8], a_tok[0:16, 4, gg * 128:(gg + 1) * 128],
                                            pmx[0:16, term, gg, 2, :]))
                        for n_, (o_, l_, r_) in enumerate(ext):
                            fns.append(lambda e, o_=o_, l_=l_, r_=r_, lastone=(n_ == len(ext) - 1): e.matmul(
                                o_, lhsT=l_, rhs=r_, start=False, stop=lastone))
                S.op("pe", fns, reads=[("a_tok", j) for j in range(5)] + ["pm_sb"], writes=[bkr])
                S.op("act", lambda e, gg=gg, bk=bk: e.activation(out=pooledT[:, gg, :], in_=bk[:], func=AF.Copy),
                     reads=[bkr], writes=[("pooledT", gg)])
                bk2_, bk2r_ = next_bank()
                S.op("pe", [lambda e, gg=gg, bk2_=bk2_: e.matmul(bk2_[:], lhsT=poolw_b[:, gg, :], rhs=pooledT[:, gg, :],
                                                                   start=True, stop=True)],
                     reads=[("pooledT", gg), "poolw_b"], writes=[bk2r_])
                S.op("dve", lambda e, gg=gg, bk2_=bk2_: e.tensor_scalar(
                    out=yabT[:, gg, :], in0=bk2_[:], scalar1=prk("pool_scale", gg), scalar2=None, op0=ALU.mult),
                    reads=[bk2r_, "PRraw"], writes=[("yab", gg)])
                yield
            for j in range(4):
                bk, bkr = next_bank()
                fns = []
                for h in range(4):
                    fns.append(lambda e, j=j, h=h, bk=bk: e.matmul(
                        bk[:, h * 128:(h + 1) * 128], lhsT=nG[:, j, h * 128:(h + 1) * 128], rhs=sguwT[:, h, :],
                        start=True, stop=False))
                    fns.append(lambda e, h=h, bk=bk: e.matmul(
                        bk[:, h * 128:(h + 1) * 128], lhsT=ident_b[:], rhs=C_hi[:, h, :], start=False, stop=False))
                    fns.append(lambda e, h=h, bk=bk: e.matmul(
                        bk[:, h * 128:(h + 1) * 128], lhsT=ident_b[:], rhs=C_lo[:, h, :], start=False, stop=False))
                    fns.append(lambda e, h=h, bk=bk: e.matmul(
                        bk[:, h * 128:(h + 1) * 128], lhsT=ident_b[:], rhs=C_l3[:, h * 128:(h + 1) * 128],
                        start=False, stop=True))
                S.op("pe", fns, reads=[("nG", j), "sguwT", "C_hi", "C_lo", "VS", "ident_b"], writes=[bkr])
                S.op("dve", lambda e, j=j, bk=bk: e.tensor_tensor(
                    out=yabT[:, 4:8, j * 128:(j + 1) * 128], in0=bk[:].rearrange("p (h c) -> p h c", h=4),
                    in1=uT[:, :, j * 128:(j + 1) * 128], op=ALU.mult),
                    reads=[bkr] + [("uT", m) for m in range(4)], writes=[("yab", 4 + h) for h in range(4)])
                yield
            if debug_taps and bi == 0:
                tap("yabT", yabT, [128, 8, T], [("yab", m) for m in range(8)])
            yield "F"
            for jj in range(2):
                wv, wr = w_get("out_%d" % jj)
                for mm in range(4):
                    m = jj * 4 + mm
                    bk, bkr = next_bank()
                    S.op("pe", [(lambda e, k=k, mm=mm, bk=bk, wv=wv: e.matmul(
                        bk[:], lhsT=wv[:, k, mm * 128:(mm + 1) * 128], rhs=yabT[:, k, :],
                        start=(k == 0), stop=(k == KC - 1))) for k in range(KC)],
                        reads=[wr] + [("yab", k) for k in range(KC)], writes=[bkr])
                    S.op("dve", lambda e, m=m, bk=bk: e.scalar_tensor_tensor(
                        out=bufA[:, m, :], in0=bk[:], scalar=prk("G_m" + sfx, m), in1=xT[:, m, HALO0:HALO0 + T],
                        op0=ALU.mult, op1=ALU.add),
                        reads=[bkr, (XT, m), ("pr", "G_m" + sfx)], writes=[("bufA", m)])
            yield
            yield from layer_norm_fm(lambda m: ("bufA", m), 1)
            affine_from_bufA("dve", lambda m: xT[:, m, HALO0:HALO0 + T], lambda m: (XT, m), "ln1_g0", "ln1_b0")
            yield
            affine_from_bufA("act", lambda m: h1T[:, m, :], lambda m: ("h1T", m), "Gp" + sfx, "Bp" + sfx)
            yield "O"
            if debug_taps and bi == 0:
                tap("x1T", xT[:, :, HALO0:HALO0 + T], [128, KC, T], [(XT, m) for m in range(KC)])
            mlp(0, sfx, lambda m: (xT[:, m, HALO0:HALO0 + T], (XT, m)))
            yield "M"
            yield from layer_norm_fm(lambda m: ("bufA", m), 1)
            affine_from_bufA("dve", lambda m: bufA[:, m, :], lambda m: ("bufA", m), "ln2_g0", "ln2_b0")
            if debug_taps and bi == 0:
                tap("x2T", bufA[:], [128, KC, T], [("bufA", m) for m in range(KC)])
            S.dma("sp", xmid[g][:, :, s:s + T], bufA[:], reads=[("bufA", m) for m in range(KC)],
                  writes=[("xmid", g, s)])


        class G:
            def __init__(self, gen):
                self.gen = gen
                self.seen = set()
                self.done = False

            def step(self):
                if self.done:
                    return None
                try:
                    v = next(self.gen)
                except StopIteration:
                    self.done = True
                    return None
                if v is not None:
                    self.seen.add(v)
                return v

            def adv(self, marker):
                while not self.done and marker not in self.seen:
                    self.step()

        def co_adv(ga, ma, na, gb, mb, nb):
            fa = lambda: ga is None or ga.done or (ma in ga.seen)
            fb = lambda: gb is None or gb.done or (mb in gb.seen)
            while not (fa() and fb()):
                for _ in range(na):
                    if fa():
                        break
                    ga.step()
                for _ in range(nb):
                    if fb():
                        break
                    gb.step()

        def advance_until(gen, marker):
            for v in gen:
                if v == marker:
                    return

        def co_advance(ga, ma, na, gb, mb, nb):
            da = ga is None
            db = gb is None
            while not (da and db):
                for _ in range(na):
                    if da:
                        break
                    try:
                        if next(ga) == ma:
                            da = True
                    except StopIteration:
                        da = True
                for _ in range(nb):
                    if db:
                        break
                    try:
                        if next(gb) == mb:
                            db = True
                    except StopIteration:
                        db = True

        blks = blocks if DEV_STAGE >= 1 else []
        if DEV_NBLK is not None:
            blks = blks[:DEV_NBLK]
        gens = [G(l0_block(bi, *blk)) for bi, blk in enumerate(blks)]
        gg = lambda i: gens[i] if i < len(gens) else None
        if gens:
            gens[0].adv("F")
        if len(gens) > 1:
            gens[1].adv("F1")
        for i in range(len(gens)):
            co_adv(gens[i], "O", 1, gg(i + 1), "F", 2)
            if gg(i + 2) is not None:
                gg(i + 2).step()
            gens[i].adv("M")
            co_adv(gens[i], "END", 1, gg(i + 2), "F1", 2)

        do_ada(1)
        do_derived(1)
        S.barrier()

        ostage = hid_f32[:, 0:4096].rearrange("p (j f) -> p j f", j=4)
        W1C = T + 2 * HALO1
        def l1_block(bi, g, s, first, last):
            sfx = "1%d" % g
            xT = xTs[bi % 2]
            XT = "xT%d" % (bi % 2)
            Sg = seqs[g]
            XR = [(XT, k) for k in range(KC)]
            lo = s - HALO1
            hi = s + T + HALO1
            c0 = max(0, -lo)
            c1 = W1C - max(0, hi - Sg)
            if first:
                S.op("pool", lambda e: e.memset(xT[:, :, 0:HALO1], 0.0), writes=XR)
            if last:
                S.op("pool", lambda e: e.memset(xT[:, :, T + HALO1:W1C], 0.0), writes=XR)
            rds = [("xmid", g, ss) for ss in range(max(0, s - T), min(Sg, s + 2 * T), T)]
            S.dma("sp", xT[:, :, c0:c1], xmid[g][:, :, lo + c0:lo + c1], reads=rds, writes=XR)
            yield
            for k in range(KC):
                for (oc, ic, n) in [(0, HALO1, T), (T, 0, HALO1), (T + HALO1, T + HALO1, HALO1)]:
                    S.op("act", lambda e, k=k, oc=oc, ic=ic, n=n: e.activation(
                        out=hT[:, k, oc:oc + n], in_=xT[:, k, ic:ic + n], func=AF.Identity,
                        scale=prk("A_m" + sfx, k), bias=prk("B_m" + sfx, k)),
                        reads=[(XT, k), ("pr", "A_m" + sfx), ("pr", "B_m" + sfx)], writes=[("hT", k)])
            HREG = [("hT", k) for k in range(KC)]
            yield
            for k in range(KC):
                S.op("dve", lambda e, k=k: e.tensor_scalar(
                    out=xT[:, k, HALO1:HALO1 + T], in0=xT[:, k, HALO1:HALO1 + T], scalar1=prk("bg" + sfx, k),
                    scalar2=None, op0=ALU.add),
                    reads=[(XT, k), ("pr", "bg" + sfx)], writes=[(XT, k)])
            yield
            for j in range(4):
                wv, wr = w_get("pw1_%d" % j)
                for i in range(2):
                    m = j * 2 + i
                    bv, bvr = next_bank()
                    bg_, bgr = next_bank()
                    bh, bhr = next_bank()
                    fns = []
                    for k in range(KC):
                        fns.append(lambda e, k=k, i=i, wv=wv, bv=bv: e.matmul(
                            bv[:], lhsT=wv[:, k, i * 128:(i + 1) * 128], rhs=hT[:, k, 0:T],
                            start=(k == 0), stop=(k == KC - 1)))
                    for k in range(KC):
                        fns.append(lambda e, k=k, i=i, wv=wv, bg_=bg_: e.matmul(
                            bg_[:], lhsT=wv[:, k, 256 + i * 128:256 + (i + 1) * 128], rhs=hT[:, k, 0:T],
                            start=(k == 0), stop=(k == KC - 1)))
                    for k in range(KC):
                        fns.append(lambda e, k=k, i=i, wv=wv, bh=bh: e.matmul(
                            bh[:, 0:2 * HALO1], lhsT=wv[:, k, i * 128:(i + 1) * 128], rhs=hT[:, k, T:T + 2 * HALO1],
                            start=(k == 0), stop=(k == KC - 1)))
                    for k in range(KC):
                        fns.append(lambda e, k=k, i=i, wv=wv, bh=bh: e.matmul(
                            bh[:, 32:32 + 2 * HALO1], lhsT=wv[:, k, 256 + i * 128:256 + (i + 1) * 128],
                            rhs=hT[:, k, T:T + 2 * HALO1], start=(k == 0), stop=(k == KC - 1)))
                    S.op("pe", fns, reads=[wr] + HREG, writes=[bvr, bgr, bhr])
                    sg = sig[:, m % 2, :]
                    S.op("act", lambda e, m=m, sg=sg, bg_=bg_: e.activation(
                        out=sg[:, 0:T], in_=bg_[:], func=AF.Sigmoid, bias=prk("pw1_b", 8 + m), scale=1.0),
                        reads=[bgr, "PRraw"], writes=[("sig", m % 2)])
                    S.op("act", lambda e, m=m, sg=sg, bh=bh: e.activation(
                        out=sg[:, T:T + 2 * HALO1], in_=bh[:, 32:32 + 2 * HALO1], func=AF.Sigmoid,
                        bias=prk("pw1_b", 8 + m), scale=1.0),
                        reads=[bhr, "PRraw", ("sig", m % 2)], writes=[("sig", m % 2)])
                    S.op("dve", lambda e, m=m, sg=sg, bv=bv: e.scalar_tensor_tensor(
                        out=gT[:, m, HALO1:HALO1 + T], in0=bv[:], scalar=prk("pw1_b", m), in1=sg[:, 0:T],
                        op0=ALU.add, op1=ALU.mult),
                        reads=[bvr, ("sig", m % 2), "PRraw"], writes=[("gT", m)])
                    S.op("dve", lambda e, m=m, sg=sg, bh=bh: e.scalar_tensor_tensor(
                        out=gT[:, m, 0:HALO1], in0=bh[:, 0:HALO1], scalar=prk("pw1_b", m), in1=sg[:, T:T + HALO1],
                        op0=ALU.add, op1=ALU.mult),
                        reads=[bhr, ("sig", m % 2), "PRraw", ("gT", m)], writes=[("gT", m)])
                    S.op("dve", lambda e, m=m, sg=sg, bh=bh: e.scalar_tensor_tensor(
                        out=gT[:, m, T + HALO1:W1C], in0=bh[:, HALO1:2 * HALO1], scalar=prk("pw1_b", m),
                        in1=sg[:, T + HALO1:T + 2 * HALO1], op0=ALU.add, op1=ALU.mult),
                        reads=[bhr, ("sig", m % 2), "PRraw", ("gT", m)], writes=[("gT", m)])
                    if first:
                        S.op("pool", lambda e, m=m: e.memset(gT[:, m, 0:HALO1], 0.0), reads=[("gT", m)], writes=[("gT", m)])
                    if last:
                        S.op("pool", lambda e, m=m: e.memset(gT[:, m, T + HALO1:W1C], 0.0), reads=[("gT", m)],
                             writes=[("gT", m)])
                    yield
            yield "F"
            def build_dg(m):
                dgs_ = dg[:, m % 2, :].rearrange("p (t c) -> p t c", t=TAPS)
                S.op("pool" if m % 2 == 0 else "dve", lambda e, m=m, dgs_=dgs_: e.tensor_tensor(
                    out=dgs_, in0=ident_b[:].unsqueeze(1).to_broadcast([128, TAPS, 128]),
                    in1=wc[:, m, :].unsqueeze(2).to_broadcast([128, TAPS, 128]), op=ALU.mult),
                    reads=["ident_b", "PRraw"], writes=[("dg", m % 2)])

            build_dg(0)
            build_dg(1)
            yield "D"
            conv_banks = []
            for m in range(KC):
                dgs = dg[:, m % 2, :].rearrange("p (t c) -> p t c", t=TAPS)
                bk, bkr = next_bank()
                S.op("pe", [(lambda e, t=t, m=m, dgs=dgs, bk=bk: e.matmul(
                    bk[:], lhsT=dgs[:, t, :], rhs=gT[:, m, t:t + T], start=(t == 0), stop=(t == TAPS - 1)))
                    for t in range(TAPS)],
                    reads=[("dg", m % 2), ("gT", m)], writes=[bkr])
                conv_banks.append((bk, bkr))
                if m + 2 < KC:
                    build_dg(m + 2)
                yield
            yield "V"
            for m in range(KC):
                bk, bkr = conv_banks[m]
                S.op("act", lambda e, m=m, bk=bk: e.activation(out=bufA[:, m, :], in_=bk[:], func=AF.Identity,
                                                             bias=prk("dw_b", m), scale=1.0),
                     reads=[bkr, "PRraw"], writes=[("bufA", m)])
            yield "E"
            if debug_taps and bi == 0:
                tap("dT", bufA[:], [128, KC, T], [("bufA", m) for m in range(KC)])
            yield from layer_norm_fm(lambda m: ("bufA", m), 0)
            for m in range(KC):
                S.op("act", lambda e, m=m: e.activation(out=sT[:, m, :], in_=bufA[:, m, :], func=AF.Silu,
                                                        scale=prk("cn_g", m), bias=prk("cn_b", m)),
                     reads=[("bufA", m), "PRraw", "PRraw"], writes=[("sT", m)])
            yield
            for jj in range(2):
                wv, wr = w_get("pw2_%d" % jj)
                for mm in range(4):
                    m = jj * 4 + mm
                    bk, bkr = next_bank()
                    S.op("pe", [(lambda e, k=k, mm=mm, bk=bk, wv=wv: e.matmul(
                        bk[:], lhsT=wv[:, k, mm * 128:(mm + 1) * 128], rhs=sT[:, k, :],
                        start=(k == 0), stop=(k == KC - 1))) for k in range(KC)],
                        reads=[wr] + [("sT", k) for k in range(KC)], writes=[bkr])
                    S.op("dve", lambda e, m=m, bk=bk: e.scalar_tensor_tensor(
                        out=bufA[:, m, :], in0=bk[:], scalar=prk("G_m" + sfx, m), in1=xT[:, m, HALO1:HALO1 + T],
                        op0=ALU.mult, op1=ALU.add),
                        reads=[bkr, (XT, m), ("pr", "G_m" + sfx)], writes=[("bufA", m)])
                    yield
            yield from layer_norm_fm(lambda m: ("bufA", m), 1)
            affine_from_bufA("dve", lambda m: xT[:, m, HALO1:HALO1 + T], lambda m: (XT, m), "ln1_g1", "ln1_b1")
            yield
            affine_from_bufA("act", lambda m: h1T[:, m, :], lambda m: ("h1T", m), "Gp" + sfx, "Bp" + sfx)
            yield "P"
            mlp(1, sfx, lambda m: (xT[:, m, HALO1:HALO1 + T], (XT, m)))
            yield "M2"
            yield from layer_norm_fm(lambda m: ("bufA", m), 1)
            affine_from_bufA("dve", lambda m: xT[:, m, HALO1:HALO1 + T], lambda m: (XT, m), "ln2_g1", "ln2_b1")
            yield "L"
            OREG = [("hid", m) for m in range(32)]
            for j in range(4):
                for half in range(2):
                    bk, bkr = next_bank()
                    S.op("pe", [(lambda e, kk=kk, j=j, half=half, bk=bk: e.transpose(
                        bk[:, kk * 128:(kk + 1) * 128], xT[:, half * 4 + kk, HALO1 + j * 128:HALO1 + (j + 1) * 128],
                        ident[:])) for kk in range(4)],
                        reads=[(XT, half * 4 + kk) for kk in range(4)] + ["ident"], writes=[bkr])
                    S.op("act" if half == 0 else "dve",
                         (lambda e, j=j, half=half, bk=bk: e.activation(
                             out=ostage[:, j, half * 512:(half + 1) * 512], in_=bk[:], func=AF.Copy))
                         if half == 0 else
                         (lambda e, j=j, half=half, bk=bk: e.tensor_copy(
                             out=ostage[:, j, half * 512:(half + 1) * 512], in_=bk[:])),
                         reads=[bkr], writes=OREG)
            S.dma("sp", y_out[g][s:s + T, :].rearrange("(j p) f -> p j f", p=128), ostage, reads=OREG)


        blks = blocks if DEV_STAGE >= 2 else []
        if DEV_NBLK is not None:
            blks = blks[:DEV_NBLK]
        gens = [G(l1_block(bi, *blk)) for bi, blk in enumerate(blks)]
        gg = lambda i: gens[i] if i < len(gens) else None
        if gens:
            gens[0].adv("F")
        for i in range(len(gens)):
            gens[i].adv("E")
            co_adv(gens[i], "P", 2, gg(i + 1), "F", 1)
            gens[i].adv("M2")
            if gg(i + 1) is not None:
                gens[i + 1].adv("D")
            co_adv(gens[i], "L", 1, gg(i + 1), "V", 1)
            if gg(i + 1) is not None:
                gens[i + 1].adv("E")
            gens[i].adv("END")

        if collect:
            return list(wstate["order"])
        S.finish()
        S.emit()
    return nc, taps, S


_PROG_CACHE = {}


def _run(inputs, seq_p, seq_s, n_cores=8, debug_taps=False):
    key = (seq_p, seq_s, debug_taps)
    if key not in _PROG_CACHE:
        _PROG_CACHE[key] = build_program(seq_p, seq_s, debug_taps)
    nc, taps, S = _PROG_CACHE[key]
    pm_p = np.ascontiguousarray(_pool_mats(seq_p))
    pm_s = np.ascontiguousarray(_pool_mats(seq_s))
    pm_x_host = np.ascontiguousarray(_pool_resid(seq_p))
    in_maps = []
    for c in range(n_cores):
        m = {
            "x_p": np.ascontiguousarray(inputs["x_prompt"][c, :seq_p]),
            "x_s": np.ascontiguousarray(inputs["x_sample"][c, :seq_s]),
            "c": np.ascontiguousarray(np.stack([inputs["c_prompt"][c], inputs["c_sample"][c]], 0)),
            "pmat_p": pm_p, "pmat_s": pm_s, "pmat_x": pm_x_host,
        }
        for n in WEIGHT_NAMES:
            m[n] = np.ascontiguousarray(inputs[n])
        in_maps.append(m)
    res = run_bass_kernel_spmd(nc, in_maps, core_ids=list(range(n_cores)))
    return res


def kernel(**inputs):
    inputs = {k: np.asarray(v) for k, v in inputs.items()}
    seq_p = inputs["x_prompt"].shape[1]
    seq_s = inputs["x_sample"].shape[1]
    res = _run(inputs, seq_p, seq_s, n_cores=8)
    y_p = np.stack([np.asarray(r["y_p"]) for r in res.results], 0).astype(np.float32)
    y_s = np.stack([np.asarray(r["y_s"]) for r in res.results], 0).astype(np.float32)
    return (y_p, y_s)
```

```python
import contextlib
import math
import numpy as np
import ml_dtypes
import concourse.bass as bass
import concourse.mybir as mybir
from concourse.bass_utils import run_bass_kernel_spmd

F32 = mybir.dt.float32
BF16 = mybir.dt.bfloat16
ALU = mybir.AluOpType
AF = mybir.ActivationFunctionType

D = 1024
KC = 8
T = 512
DFF = 4096
DEPTH = 2
ALPHA = (2.0 * DEPTH) ** 0.25
LN_EPS = 1e-5
EPS_POST = LN_EPS / (ALPHA * ALPHA)
POOL_WINDOWS = (2, 4, 8, 16)
TAPS = 31
HALO0 = 8
HALO1 = 15
N_DMA_SEMS = 24
N_WSLOTS = 3

WEIGHT_NAMES = [
    "l0_ada_w", "l0_ada_b", "l0_in_w", "l0_pool_w", "l0_pool_scale", "l0_sgu_ln_g", "l0_sgu_ln_b",
    "l0_sgu_w", "l0_sgu_b", "l0_out_w", "l0_ln1_g", "l0_ln1_b", "l0_mlp_w1", "l0_mlp_w2", "l0_ln2_g",
    "l0_ln2_b",
    "l1_ada_w", "l1_ada_b", "l1_pw1_w", "l1_pw1_b", "l1_dw_w", "l1_dw_b", "l1_cnorm_g", "l1_cnorm_b",
    "l1_pw2_w", "l1_pw2_b", "l1_ln1_g", "l1_ln1_b", "l1_mlp_w1", "l1_mlp_w2", "l1_ln2_g", "l1_ln2_b",
]


class _Rec:
    def __init__(self):
        self.call = None

    def __getattr__(self, name):
        def f(*a, **kw):
            self.call = (name, a, kw)
            return self
        return f


class Sched:
    def __init__(self, nc, stack):
        self.nc = nc
        self.engs = ["pe", "act", "dve", "pool", "sp"]
        self.ops = {e: [] for e in self.engs}
        self.sem = {e: stack.enter_context(nc.semaphore("s_" + e)) for e in self.engs}
        self.count = {e: 0 for e in self.engs}
        self.dsem = [stack.enter_context(nc.semaphore("d%d" % i)) for i in range(N_DMA_SEMS)]
        self.dval = [0] * N_DMA_SEMS
        self.dnext = 0
        self.waited = {e: {} for e in self.engs}
        self.regions = {}
        self.n_waits = 0
        self.n_ops = {e: 0 for e in self.engs}

    def _reg(self, r):
        if r not in self.regions:
            self.regions[r] = {"w": None, "r": {}}
        return self.regions[r]

    def _need(self, eng, reads, writes):
        need = {}

        def add(tok):
            if tok is None:
                return
            k, v = tok
            if k == ("e", "pe") and eng == "pe":
                return
            if need.get(k, 0) < v:
                need[k] = v

        for r in reads:
            add(self._reg(r)["w"])
        for w in writes:
            rg = self._reg(w)
            add(rg["w"])
            for k, v in rg["r"].items():
                if k == ("e", eng):
                    continue
                add((k, v))
        return need

    def _emit_waits(self, eng, need):
        for k, v in need.items():
            if self.waited[eng].get(k, 0) >= v:
                continue
            self.waited[eng][k] = v
            semh = self.sem[k[1]] if k[0] == "e" else self.dsem[k[1]]
            self.ops[eng].append(lambda e, semh=semh, v=v: e.wait_ge(semh, v))
            self.n_waits += 1

    def _commit(self, tok, reads, writes):
        k, v = tok
        for r in reads:
            self._reg(r)["r"][k] = v
        for w in writes:
            rg = self._reg(w)
            rg["w"] = tok
            rg["r"] = {}

    def op(self, eng, fns, reads=(), writes=()):
        if callable(fns):
            fns = [fns]
        calls = []
        for f in fns:
            r = _Rec()
            f(r)
            assert r.call is not None
            calls.append(r.call)
        bank_rd = [r for r in reads if isinstance(r, tuple) and r[0] == "bank"]
        if bank_rd:
            writes = list(writes) + bank_rd
        need = self._need(eng, reads, writes)
        self._emit_waits(eng, need)
        self.count[eng] += 1
        v = self.count[eng]
        semh = self.sem[eng]
        for (name, a, kw) in calls[:-1]:
            self.ops[eng].append(lambda e, name=name, a=a, kw=kw: getattr(e, name)(*a, **kw))
        name, a, kw = calls[-1]
        self.ops[eng].append(lambda e, name=name, a=a, kw=kw, semh=semh: getattr(e, name)(*a, **kw).then_inc(semh, 1))
        self.n_ops[eng] += len(fns)
        self._commit((("e", eng), v), reads, writes)

    def dma_once(self, q, out, in_, stack, reads=(), writes=(), **kw):
        i = len(self.dsem)
        self.dsem.append(stack.enter_context(self.nc.semaphore("c%d" % i)))
        self.dval.append(0)
        need = self._need(q, reads, writes)
        self._emit_waits(q, need)
        self.dval[i] = 16
        semh = self.dsem[i]
        self.ops[q].append(lambda e, out=out, in_=in_, semh=semh, kw=kw:
                           e.dma_start(out=out, in_=in_, **kw).then_inc(semh, 16))
        self._commit((("d", i), 16), reads, writes)

    def dma(self, q, out, in_, reads=(), writes=(), **kw):
        i = self.dnext
        self.dnext = (self.dnext + 1) % N_DMA_SEMS
        need = self._need(q, reads, writes)
        if self.dval[i] > 0:
            k = ("d", i)
            if need.get(k, 0) < self.dval[i]:
                need[k] = self.dval[i]
        self._emit_waits(q, need)
        self.dval[i] += 16
        v = self.dval[i]
        semh = self.dsem[i]
        self.ops[q].append(lambda e, out=out, in_=in_, semh=semh, kw=kw:
                           e.dma_start(out=out, in_=in_, **kw).then_inc(semh, 16))
        self._commit((("d", i), v), reads, writes)

    def barrier(self):
        need = {}
        for i in range(len(self.dval)):
            if self.dval[i] > 0:
                need[("d", i)] = self.dval[i]
        for e in self.engs:
            if self.count[e] > 0:
                need[("e", e)] = self.count[e]
        for e in self.engs:
            n2 = {k: v for k, v in need.items() if k != ("e", e)}
            self._emit_waits(e, n2)

    def finish(self):
        need = {}
        for i in range(len(self.dval)):
            if self.dval[i] > 0:
                need[("d", i)] = self.dval[i]
        for e in ["pe", "act", "dve", "pool"]:
            if self.count[e] > 0:
                need[("e", e)] = self.count[e]
        self._emit_waits("sp", need)

    def emit(self):
        nc = self.nc
        ops = self.ops
        with nc.Block() as block:
            @block.sync
            def _(e):
                for f in ops["sp"]:
                    f(e)

            @block.tensor
            def _(e):
                for f in ops["pe"]:
                    f(e)

            @block.scalar
            def _(e):
                for f in ops["act"]:
                    f(e)

            @block.vector
            def _(e):
                for f in ops["dve"]:
                    f(e)

            @block.gpsimd
            def _(e):
                for f in ops["pool"]:
                    f(e)


def _pool_mats(S):
    out = np.zeros((5, 128, 4, 128), np.float32)
    big = 1 << 20

    def fill(dst, tin, tout, S_):
        for g, w in enumerate(POOL_WINDOWS):
            lo = np.clip(tout - w // 2, 0, S_)
            hi = np.clip(tout + w // 2, 0, S_)
            cnt = (hi - lo).astype(np.float32)
            ti = tin[:, None]
            inside = (ti >= lo[None, :]) & (ti < hi[None, :]) & (ti >= 0) & (ti < S_)
            m = inside.astype(np.float32) / cnt[None, :]
            m -= ((ti == tout[None, :]) & (ti >= 0) & (ti < S_)).astype(np.float32)
            dst[:, g, :] = m

    r = np.arange(128)
    c = np.arange(128)
    base = 4096
    fill(out[0], base - 8 + r, base + c, big)
    fill(out[3], base + 120 + r, base + c, big)
    out[3][16:] = 0.0
    fill(out[1], r - 8, c, big)
    fill(out[2], S - 128 - 8 + r, S - 128 + c, S)
    fill(out[4], S - 8 + r, S - 128 + c, S)
    out[4][16:] = 0.0
    return out.astype(ml_dtypes.bfloat16)


def _pool_resid(S):
    out = np.zeros((128, 2, 4, 3, 8), np.float32)
    big = 1 << 20
    r = np.arange(128)
    c = np.arange(128)

    def fullmat(tin, tout, S_):
        m = np.zeros((128, 4, 128), np.float32)
        for g, w in enumerate(POOL_WINDOWS):
            lo = np.clip(tout - w // 2, 0, S_)
            hi = np.clip(tout + w // 2, 0, S_)
            cnt = (hi - lo).astype(np.float32)
            ti = tin[:, None]
            inside = (ti >= lo[None, :]) & (ti < hi[None, :]) & (ti >= 0) & (ti < S_)
            mm = inside.astype(np.float32) / cnt[None, :]
            mm -= ((ti == tout[None, :]) & (ti >= 0) & (ti < S_)).astype(np.float32)
            m[:, g, :] = mm
        return m

    mats = [
        (fullmat(r - 8, c, big), slice(0, 8)),
        (fullmat(S - 128 - 8 + r, S - 128 + c, S), slice(120, 128)),
        (fullmat(S - 8 + r, S - 128 + c, S), slice(120, 128)),
    ]
    mats[2][0][16:] = 0.0
    bf = lambda a: a.astype(ml_dtypes.bfloat16).astype(np.float32)
    for v, (m, cs) in enumerate(mats):
        res = m - bf(m)
        mid = bf(res)
        lo_ = bf(res - mid)
        assert np.abs(res[:, :, [k for k in range(128) if not (cs.start <= k < cs.stop)]]).max() == 0.0
        out[:, 0, :, v, :] = mid[:, :, cs]
        out[:, 1, :, v, :] = lo_[:, :, cs]
    return out.astype(ml_dtypes.bfloat16)


DEV_STAGE = 2
DEV_NBLK = None
DEV_CUT = 99
DEV_NOCONVPRE = False


class _Cut(Exception):
    pass


def build_program(seq_p, seq_s, debug_taps=False, _order=None):
    if _order is None:
        _order = build_program(seq_p, seq_s, debug_taps, _order="collect")
    collect = (_order == "collect")
    assert seq_p % T == 0 and seq_s % T == 0 and seq_p >= 2 * T and seq_s >= 2 * T
    nc = bass.Bass("TRN2", target_bir_lowering=False)
    seqs = [seq_p, seq_s]

    def din(name, shape, dt=F32):
        return nc.dram_tensor(name, list(shape), dt, kind="ExternalInput").ap()

    def dint(name, shape, dt):
        return nc.dram_tensor(name, list(shape), dt, kind="Internal").ap()

    x_in = [din("x_p", [seq_p, D]), din("x_s", [seq_s, D])]
    c_in = din("c", [2, D])
    shapes = {
        "l0_ada_w": [D, 6 * D], "l0_ada_b": [6 * D], "l0_in_w": [D, 1536], "l0_pool_w": [4, 128, 128],
        "l0_pool_scale": [512], "l0_sgu_ln_g": [512], "l0_sgu_ln_b": [512], "l0_sgu_w": [4, 128, 128],
        "l0_sgu_b": [4, 128], "l0_out_w": [D, D], "l0_ln1_g": [D], "l0_ln1_b": [D],
        "l0_mlp_w1": [D, DFF], "l0_mlp_w2": [DFF, D], "l0_ln2_g": [D], "l0_ln2_b": [D],
        "l1_ada_w": [D, 6 * D], "l1_ada_b": [6 * D], "l1_pw1_w": [D, 2 * D], "l1_pw1_b": [2 * D],
        "l1_dw_w": [TAPS, D], "l1_dw_b": [D], "l1_cnorm_g": [D], "l1_cnorm_b": [D],
        "l1_pw2_w": [D, D], "l1_pw2_b": [D], "l1_ln1_g": [D], "l1_ln1_b": [D],
        "l1_mlp_w1": [D, DFF], "l1_mlp_w2": [DFF, D], "l1_ln2_g": [D], "l1_ln2_b": [D],
    }
    W = {n: din(n, shapes[n]) for n in WEIGHT_NAMES}
    pm_in = [din("pmat_p", [5, 128, 4, 128], BF16), din("pmat_s", [5, 128, 4, 128], BF16)]
    pmx_in = din("pmat_x", [128, 2, 4, 3, 8], BF16)
    y_out = [nc.dram_tensor("y_p", [seq_p, D], F32, kind="ExternalOutput").ap(),
             nc.dram_tensor("y_s", [seq_s, D], F32, kind="ExternalOutput").ap()]
    xmid = [dint("xmid_p", [128, KC, seq_p], F32), dint("xmid_s", [128, KC, seq_s], F32)]
    taps = {}

    with contextlib.ExitStack() as st:
        S = Sched(nc, st)

        def sb(name, shape, dt):
            return st.enter_context(nc.sbuf_tensor(name, list(shape), dt))

        banks = [st.enter_context(nc.psum_tensor("bank%d" % i, [128, 512], F32)) for i in range(8)]
        bstate = {"n": 0}

        def next_bank():
            i = bstate["n"] % 8
            bstate["n"] += 1
            return banks[i], ("bank", i)

        pieces = {}
        cast_jobs = []

        def add_piece(pname, srcs, kc, cols):
            ap = dint("wp_" + pname, [128, kc, cols], BF16)
            pieces[pname] = (ap, kc, cols)
            cast_jobs.append((pname, srcs))

        def wsrc(wname, c0, c1):
            return W[wname].rearrange("(k p) c -> p k c", p=128)[:, :, c0:c1]

        for L in range(2):
            for j in range(12):
                add_piece("ada%d_%d" % (L, j), [(0, 512, wsrc("l%d_ada_w" % L, j * 512, (j + 1) * 512))], 8, 512)
        for j in range(3):
            add_piece("in_%d" % j, [(0, 512, wsrc("l0_in_w", j * 512, (j + 1) * 512))], 8, 512)
        for j in range(2):
            add_piece("out_%d" % j, [(0, 512, wsrc("l0_out_w", j * 512, (j + 1) * 512))], 8, 512)
        for L in range(2):
            for j in range(8):
                add_piece("w1_%d_%d" % (L, j), [(0, 512, wsrc("l%d_mlp_w1" % L, j * 512, (j + 1) * 512))], 8, 512)
            for j in range(8):
                add_piece("w2_%d_%d" % (L, j), [(0, 128, wsrc("l%d_mlp_w2" % L, j * 128, (j + 1) * 128))], 32, 128)
        for j in range(4):
            add_piece("pw1_%d" % j, [(0, 256, wsrc("l1_pw1_w", j * 256, (j + 1) * 256)),
                                     (256, 512, wsrc("l1_pw1_w", D + j * 256, D + (j + 1) * 256))], 8, 512)
        for j in range(2):
            add_piece("pw2_%d" % j, [(0, 512, wsrc("l1_pw2_w", j * 512, (j + 1) * 512))], 8, 512)

        def issue_casts(names):
            for pname in names:
                srcs = dict(cast_jobs)[pname]
                ap = pieces[pname][0]
                for (c0, c1, src) in srcs:
                    S.dma_once("pool", ap[:, :, c0:c1], src, st, reads=[], writes=[("piece", pname, c0)])

        wslots = [sb("wslot%d" % i, [128, 4096], BF16) for i in range(N_WSLOTS)]
        wstate = {"order": [], "issued": 0, "got": 0}

        def piece_regions(pname):
            return [("piece", pname, c0) for (c0, _, _) in dict(cast_jobs)[pname]]

        def w_issue_upto(n):
            while wstate["issued"] < min(n, len(wstate["order"])):
                i = wstate["issued"]
                pname = wstate["order"][i]
                ap, kc, cols = pieces[pname]
                slot = wslots[i % N_WSLOTS]
                S.dma("sp", slot[:, 0:kc * cols].rearrange("p (k c) -> p k c", k=kc), ap,
                      reads=piece_regions(pname), writes=[("wslot", i % N_WSLOTS)])
                wstate["issued"] += 1

        def w_get(pname):
            i = wstate["got"]
            if collect:
                wstate["order"].append(pname)
            assert wstate["order"][i] == pname, (wstate["order"][i], pname)
            w_issue_upto(i + N_WSLOTS - 1)
            wstate["got"] += 1
            ap, kc, cols = pieces[pname]
            slot = wslots[i % N_WSLOTS]
            return slot[:, 0:kc * cols].rearrange("p (k c) -> p k c", k=kc), ("wslot", i % N_WSLOTS)

        order = []
        for L in range(2):
            order += ["ada%d_%d" % (L, j) for j in range(12)]
        l0_order = ["in_0", "in_2", "in_1", "out_0", "out_1"] + ["w1_0_%d" % j for j in range(8)] + \
                   ["w2_0_%d" % j for j in range(8)]
        l1_order = ["pw1_%d" % j for j in range(4)] + ["pw2_0", "pw2_1"] + ["w1_1_%d" % j for j in range(8)] + \
                   ["w2_1_%d" % j for j in range(8)]
        blocks = []
        for g in range(2):
            nb = seqs[g] // T
            for b in range(nb):
                blocks.append((g, b * T, b == 0, b == nb - 1))
        for _ in blocks:
            order += l0_order
        for _ in blocks:
            order += l1_order
        wstate["order"] = [] if collect else list(_order)

        ident = sb("ident", [128, 128], F32)
        ident_b = sb("ident_b", [128, 128], BF16)
        ones_b = sb("ones_b", [128, 128], BF16)
        mean_b = sb("mean_b", [128, 128], BF16)
        eps_t = sb("eps_t", [128, 2], F32)
        NPAR = 1024
        PR = sb("PR", [128, NPAR], F32)
        prcol = {"n": 0}
        prmap = {}

        def pr_alloc(name, n):
            o = prcol["n"]
            prcol["n"] += n
            assert prcol["n"] <= NPAR
            prmap[name] = (o, n)
            return PR[:, o:o + n]

        def pr(name):
            o, n = prmap[name]
            return PR[:, o:o + n]

        def prk(name, k):
            o, n = prmap[name]
            return PR[:, o + k:o + k + 1]

        mod = [sb("mod%d" % L, [128, 48, 2], F32) for L in range(2)]
        cT = sb("cT", [128, KC, 2], F32)
        sT_c = sb("sT_c", [128, KC, 2], BF16)

        xTs = [sb("xT%d" % i, [128, KC, 544], F32) for i in range(2)]
        bufA = sb("bufA", [128, KC, T], F32)
        hT = sb("hT", [128, KC, 544], BF16)
        h1T = sb("h1T", [128, KC, T], BF16)
        vb = sb("vb", [128, KC, T], BF16)
        v2b = sb("v2b", [128, KC, T], BF16)
        stats = sb("stats", [128, 4, T], F32)
        hidT = sb("hidT", [128, 32, T], BF16)
        hid_f32 = hidT[:].rearrange("p a b -> p (a b)").bitcast(F32)
        relu_t = [sb("relu%d" % i, [128, T], BF16) for i in range(3)]
        SCR_F32 = 9600
        scr = sb("scr", [128, SCR_F32], F32)
        pm_sb = sb("pm_sb", [128, 5, 512], BF16)
        pm_x = sb("pm_x", [128, 2 * 4 * 3 * 8], BF16)
        sguwT = sb("sguwT", [128, 4, 128], BF16)
        C_hi = sb("C_hi", [128, 4, 128], BF16)
        C_lo = sb("C_lo", [128, 4, 128], BF16)
        poolw_b = sb("poolw_b", [128, 4, 128], BF16)
        g_bc = sb("g_bc", [128, 512], F32)

        def carve(off_f32, shape, dt):
            n = int(np.prod(shape[1:]))
            nf = n if dt == F32 else (n + 1) // 2
            v = scr[:, off_f32:off_f32 + nf]
            if dt != F32:
                v = v.bitcast(dt)
            if len(shape) == 3:
                v = v.rearrange("p (a b) -> p a b", a=shape[1])
            return v, off_f32 + nf

        o = 0
        a_tok, o = carve(o, [128, 5, 512], BF16)
        uT, o = carve(o, [128, 4, T], F32)
        v_tok, o = carve(o, [128, 4, 512], F32)
        nG, o = carve(o, [128, 4, 512], BF16)
        pooledT, o = carve(o, [128, 4, T], BF16)
        yabT, o = carve(o, [128, 8, T], BF16)
        assert o <= SCR_F32, o
        o = 0
        gT, o = carve(o, [128, KC, 544], BF16)
        sig, o = carve(o, [128, 2, 544], F32)
        sT, o = carve(o, [128, KC, T], BF16)
        dg, o = carve(o, [128, 2, TAPS * 128], BF16)
        assert o <= SCR_F32, o

        small = sb("small", [128, 64], F32)

        def tap(name, ap, shape, reads):
            if not debug_taps:
                return
            t = nc.dram_tensor("tap_" + name, list(shape), ap.dtype, kind="ExternalOutput").ap()
            taps[name] = t
            S.dma("sp", t, ap, reads=reads)

        S.op("pool", lambda e: e.memset(ident[:], 0.0), writes=["ident"])
        S.op("pool", lambda e: e.affine_select(out=ident[:], in_=ident[:], pattern=[[-1, 128]],
                                                compare_op=ALU.not_equal, fill=1.0, base=0,
                                                channel_multiplier=1), reads=["ident"], writes=["ident"])
        S.op("pool", lambda e: e.tensor_copy(out=ident_b[:], in_=ident[:]), reads=["ident"], writes=["ident_b"])
        S.op("pool", lambda e: e.memset(ones_b[:], 1.0), writes=["ones_b"])
        S.op("pool", lambda e: e.memset(mean_b[:], 1.0 / D), writes=["mean_b"])
        S.op("pool", lambda e: e.memset(eps_t[:, 0:1], LN_EPS), writes=["eps_t"])
        S.op("pool", lambda e: e.memset(eps_t[:, 1:2], EPS_POST), reads=["eps_t"], writes=["eps_t"])

        issue_casts(order[:12])
        issue_casts(l0_order)
        issue_casts(order[12:24])
        issue_casts(l1_order)

        VS = sb("VS", [128, 4, 128], F32)
        S.op("dve", lambda e: e.memset(VS[:].rearrange("p a b -> p (a b)"), 0.0), writes=["VS"])
        vrow = {"n": 0}

        def load_vec(name, src_ap, n):
            if (vrow["n"] % 128) + n > 128:
                vrow["n"] = (vrow["n"] // 128 + 1) * 128
            r = vrow["n"]
            vrow["n"] += n
            assert vrow["n"] <= 512
            prmap[name] = (r, n)
            S.dma("sp", VS[r % 128:r % 128 + n, r // 128, :], src_ap.rearrange("(k p) -> k p", p=128),
                  reads=[], writes=["VS"])

        for L in range(2):
            load_vec("ada_b%d" % L, W["l%d_ada_b" % L], 48)
        for L in range(2):
            for nm in ["ln1_g", "ln1_b", "ln2_g", "ln2_b"]:
                load_vec("%s%d" % (nm, L), W["l%d_%s" % (L, nm)], 8)
        load_vec("pool_scale", W["l0_pool_scale"], 4)
        load_vec("sgu_ln_b", W["l0_sgu_ln_b"], 4)
        load_vec("pw1_b", W["l1_pw1_b"], 16)
        load_vec("dw_b", W["l1_dw_b"], 8)
        load_vec("cn_g", W["l1_cnorm_g"], 8)
        load_vec("cn_b", W["l1_cnorm_b"], 8)
        load_vec("pw2_b", W["l1_pw2_b"], 8)
        load_vec("c0", c_in[0], 8)
        load_vec("c1", c_in[1], 8)
        vrow["n"] = 256
        load_vec("dww_a", W["l1_dw_w"][0:16, :].rearrange("t f -> (t f)"), 128)
        load_vec("dww_b", W["l1_dw_w"][16:31, :].rearrange("t f -> (t f)"), 120)
        prcol["n"] = 512
        for i in range(4):
            bkv, bkvr = next_bank()
            S.op("pe", [lambda e, i=i, bkv=bkv: e.transpose(bkv[:, 0:128], VS[:, i, :], ident[:])],
                 reads=["VS", "ident"], writes=[bkvr])
            S.op("dve", lambda e, i=i, bkv=bkv: e.tensor_copy(out=PR[:, i * 128:(i + 1) * 128], in_=bkv[:, 0:128]),
                 reads=[bkvr], writes=["PRraw"])
        wc = PR[:, 256:256 + TAPS * 8].rearrange("p (t k) -> p k t", k=8)
        for g in range(2):
            S.op("dve", lambda e, g=g: e.tensor_copy(out=cT[:, :, g], in_=pr("c%d" % g)), reads=["PRraw"], writes=["cT"])
        S.dma("sp", g_bc[:], W["l0_sgu_ln_g"].partition_broadcast(128), writes=["g_bc"])
        S.dma("sp", pm_sb[:], pm_in[0].rearrange("v p g c -> p v (g c)"), writes=["pm_sb"])
        S.dma("sp", pm_x[:], pmx_in.rearrange("p a g v c -> p (a g v c)"), writes=["pm_sb"])
        pw_f = hid_f32[:, 0:512].rearrange("p (g e) -> p g e", g=4)
        S.dma("sp", pw_f, W["l0_pool_w"].rearrange("g d e -> d g e"), writes=[("hid", 0)])
        S.op("dve", lambda e: e.tensor_copy(out=poolw_b[:], in_=pw_f), reads=[("hid", 0)], writes=["poolw_b"])
        sw_f = hid_f32[:, 512:1024].rearrange("p (h q) -> p h q", h=4)
        S.dma("sp", sw_f, W["l0_sgu_w"].rearrange("h p q -> p h q"), writes=[("hid", 0)])
        bk, bkr = next_bank()
        S.op("pe", [(lambda e, h=h: e.transpose(bk[:, h * 128:(h + 1) * 128], sw_f[:, h, :], ident[:]))
                    for h in range(4)], reads=[("hid", 0), "ident"], writes=[bkr])
        S.op("dve", lambda e: e.tensor_copy(out=sguwT[:].rearrange("p h q -> p (h q)"), in_=bk[:]),
             reads=[bkr], writes=["sguwT"])
        bk2, bk2r = next_bank()
        S.op("pe", [lambda e: e.matmul(bk2[:], lhsT=ones_b[:], rhs=sguwT[:].rearrange("p h q -> p (h q)"),
                                       start=True, stop=True)], reads=["ones_b", "sguwT"], writes=[bk2r])
        sgub_bc = hid_f32[:, 1024:1536]
        S.dma("sp", sgub_bc, W["l0_sgu_b"].rearrange("h p -> (h p)").partition_broadcast(128), writes=[("hid", 0)])
        Cf = hid_f32[:, 1536:2048]
        Ct = hid_f32[:, 2048:2560]
        for h in range(4):
            S.op("dve", lambda e, h=h: e.scalar_tensor_tensor(
                out=Cf[:, h * 128:(h + 1) * 128], in0=bk2[:, h * 128:(h + 1) * 128],
                scalar=prk("sgu_ln_b", h), in1=sgub_bc[:, h * 128:(h + 1) * 128],
                op0=ALU.mult, op1=ALU.add),
                reads=[bk2r, ("hid", 0), "PRraw"], writes=[("hid", 0)])
        S.op("dve", lambda e: e.tensor_copy(out=C_hi[:].rearrange("p h q -> p (h q)"), in_=Cf), reads=[("hid", 0)], writes=["C_hi"])
        S.op("dve", lambda e: e.tensor_tensor(out=Ct, in0=Cf, in1=C_hi[:].rearrange("p h q -> p (h q)"), op=ALU.subtract),
             reads=[("hid", 0), "C_hi"], writes=[("hid", 0)])
        S.op("dve", lambda e: e.tensor_copy(out=C_lo[:].rearrange("p h q -> p (h q)"), in_=Ct), reads=[("hid", 0)], writes=["C_lo"])
        C_l3 = VS[:].rearrange("p a b -> p (a b)").bitcast(BF16)[:, 0:512]
        S.op("dve", lambda e: e.tensor_tensor(out=Cf, in0=Ct, in1=C_lo[:].rearrange("p h q -> p (h q)"), op=ALU.subtract),
             reads=[("hid", 0), "C_lo"], writes=[("hid", 0)])
        S.op("dve", lambda e: e.tensor_copy(out=C_l3, in_=Cf), reads=[("hid", 0), "VS"], writes=["VS"])

        S.op("act", lambda e: e.activation(out=sT_c[:].rearrange("p k g -> p (k g)"),
                                           in_=cT[:].rearrange("p k g -> p (k g)"), func=AF.Silu),
             reads=["cT"], writes=["sT_c"])
        def do_ada(L):
            bkm, bkmr = next_bank()
            for j in range(12):
                wv, wr = w_get("ada%d_%d" % (L, j))
                fns = []
                for mm in range(4):
                    m = j * 4 + mm
                    for k in range(KC):
                        fns.append(lambda e, m=m, mm=mm, k=k, wv=wv: e.matmul(
                            bkm[:, m * 2:m * 2 + 2], lhsT=wv[:, k, mm * 128:(mm + 1) * 128], rhs=sT_c[:, k, :],
                            start=(k == 0), stop=(k == KC - 1)))
                S.op("pe", fns, reads=[wr, "sT_c"], writes=[bkmr])
                if debug_taps and L == 0 and j in (0, 11):
                    tap("slot%d" % j, wv, [128, 8, 512], [wr])
            if debug_taps and L == 0:
                S.op("dve", lambda e, bkm=bkm: e.tensor_copy(out=stats[:, 0, 0:96], in_=bkm[:, 0:96]), reads=[bkmr], writes=[("stats", 0)])
                tap("praw", stats[:, 0, 0:96], [128, 96], [("stats", 0)])
            S.op("dve", lambda e, L=L, bkm=bkm: e.tensor_tensor(
                out=mod[L][:], in0=bkm[:, 0:96].rearrange("p (m g) -> p m g", g=2),
                in1=pr("ada_b%d" % L).unsqueeze(2).to_broadcast([128, 48, 2]), op=ALU.add),
                reads=[bkmr, "PRraw"], writes=[("mod", L)])

        def modv(L, which, g):
            return mod[L][:, which * 8:(which + 1) * 8, g]

        def do_derived(L):
            for g in range(2):
                sfx = "%d%d" % (L, g)
                rd = [("mod", L)]
                for nm, which in [("A_m", 1), ("A_f", 4)]:
                    dst = pr_alloc(nm + sfx, 8)
                    S.op("dve", lambda e, dst=dst, L=L, which=which, g=g: e.tensor_scalar(
                        out=dst, in0=modv(L, which, g), scalar1=1.0, scalar2=None, op0=ALU.add),
                        reads=rd, writes=[("pr", nm + sfx)])
                for nm, which in [("B_m", 0), ("B_f", 3)]:
                    dst = pr_alloc(nm + sfx, 8)
                    S.op("dve", lambda e, dst=dst, L=L, which=which, g=g: e.tensor_copy(
                        out=dst, in_=modv(L, which, g)), reads=rd, writes=[("pr", nm + sfx)])
                for nm, which in [("G_m", 2), ("G_f", 5)]:
                    dst = pr_alloc(nm + sfx, 8)
                    S.op("dve", lambda e, dst=dst, L=L, which=which, g=g: e.tensor_scalar(
                        out=dst, in0=modv(L, which, g), scalar1=1.0 / ALPHA, scalar2=None, op0=ALU.mult),
                        reads=rd, writes=[("pr", nm + sfx)])
                dst = pr_alloc("Gp" + sfx, 8)
                S.op("dve", lambda e, dst=dst, L=L, sfx=sfx: e.tensor_tensor(
                    out=dst, in0=pr("ln1_g%d" % L), in1=pr("A_f" + sfx), op=ALU.mult),
                    reads=["PRraw", ("pr", "A_f" + sfx)], writes=[("pr", "Gp" + sfx)])
                dst = pr_alloc("Bp" + sfx, 8)
                S.op("dve", lambda e, dst=dst, L=L, sfx=sfx: e.tensor_tensor(
                    out=dst, in0=pr("ln1_b%d" % L), in1=pr("A_f" + sfx), op=ALU.mult),
                    reads=["PRraw", ("pr", "A_f" + sfx)], writes=[("pr", "Bp" + sfx)])
                S.op("dve", lambda e, dst=dst, sfx=sfx: e.tensor_tensor(
                    out=dst, in0=dst, in1=pr("B_f" + sfx), op=ALU.add),
                    reads=[("pr", "Bp" + sfx), ("pr", "B_f" + sfx)], writes=[("pr", "Bp" + sfx)])
                if L == 1:
                    dst = pr_alloc("bg" + sfx, 8)
                    S.op("dve", lambda e, dst=dst, sfx=sfx: e.tensor_tensor(
                        out=dst, in0=pr("pw2_b"), in1=pr("G_m" + sfx), op=ALU.mult),
                        reads=["PRraw", ("pr", "G_m" + sfx)], writes=[("pr", "bg" + sfx)])

        do_ada(0)
        do_derived(0)

        def layer_norm_fm(vreg, eps_col):
            bm, bmr = next_bank()
            bq, bqr = next_bank()
            for k in range(KC):
                S.op("act", lambda e, k=k: e.activation(out=vb[:, k, :], in_=bufA[:, k, :], func=AF.Copy),
                     reads=[vreg(k)], writes=[("vb", k)])
                S.op("act", lambda e, k=k: e.activation(out=v2b[:, k, :], in_=bufA[:, k, :], func=AF.Square),
                     reads=[vreg(k)], writes=[("v2b", k)])
            for k in range(KC):
                S.op("pe", [lambda e, k=k: e.matmul(bm[:], lhsT=mean_b[:], rhs=vb[:, k, :],
                                                    start=(k == 0), stop=(k == KC - 1))],
                     reads=[("vb", k), "mean_b"], writes=[bmr])
            for k in range(KC):
                S.op("pe", [lambda e, k=k: e.matmul(bq[:], lhsT=mean_b[:], rhs=v2b[:, k, :],
                                                    start=(k == 0), stop=(k == KC - 1))],
                     reads=[("v2b", k), "mean_b"], writes=[bqr])
            yield
            S.op("act", lambda e: e.activation(out=stats[:, 0, :], in_=bm[:], func=AF.Square),
                 reads=[bmr], writes=[("stats", 0)])
            S.op("dve", lambda e: e.tensor_tensor(out=stats[:, 1, :], in0=bq[:], in1=stats[:, 0, :], op=ALU.subtract),
                 reads=[bqr, ("stats", 0)], writes=[("stats", 1)])
            S.op("dve", lambda e: e.tensor_scalar(out=stats[:, 1, :], in0=stats[:, 1, :], scalar1=0.0, scalar2=None,
                                                  op0=ALU.max), reads=[("stats", 1)], writes=[("stats", 1)])
            S.op("act", lambda e: e.activation(out=stats[:, 1, :], in_=stats[:, 1, :], func=AF.Sqrt,
                                               bias=eps_t[:, eps_col:eps_col + 1], scale=1.0),
                 reads=[("stats", 1), "eps_t"], writes=[("stats", 1)])
            S.op("dve", lambda e: e.reciprocal(out=stats[:, 2, :], in_=stats[:, 1, :]),
                 reads=[("stats", 1)], writes=[("stats", 2)])
            S.op("dve", lambda e: e.tensor_tensor(out=stats[:, 3, :], in0=bm[:], in1=stats[:, 2, :], op=ALU.mult),
                 reads=[bmr, ("stats", 2)], writes=[("stats", 3)])
            yield
            for k in range(KC):
                S.op("dve", lambda e, k=k: e.tensor_tensor(out=bufA[:, k, :], in0=bufA[:, k, :], in1=stats[:, 2, :],
                                                           op=ALU.mult),
                     reads=[vreg(k), ("stats", 2)], writes=[vreg(k)])
                S.op("dve", lambda e, k=k: e.tensor_tensor(out=bufA[:, k, :], in0=bufA[:, k, :], in1=stats[:, 3, :],
                                                           op=ALU.subtract),
                     reads=[vreg(k), ("stats", 3)], writes=[vreg(k)])
                if k % 2 == 1:
                    yield

        def run(gen):
            for _ in gen:
                pass

        def interleave(ga, gb):
            da = db = False
            while not (da and db):
                if not da:
                    try:
                        next(ga)
                    except StopIteration:
                        da = True
                if not db:
                    try:
                        next(gb)
                    except StopIteration:
                        db = True

        def mlp(L, sfx, xres):
            for j in range(8):
                wv, wr = w_get("w1_%d_%d" % (L, j))
                for mm in range(4):
                    m = j * 4 + mm
                    bk, bkr = next_bank()
                    S.op("pe", [(lambda e, k=k, mm=mm, wv=wv, bk=bk: e.matmul(
                        bk[:], lhsT=wv[:, k, mm * 128:(mm + 1) * 128], rhs=h1T[:, k, :],
                        start=(k == 0), stop=(k == KC - 1))) for k in range(KC)],
                        reads=[wr] + [("h1T", k) for k in range(KC)], writes=[bkr])
                    rt = relu_t[m % 3]
                    S.op("act", lambda e, rt=rt, bk=bk: e.activation(out=rt[:], in_=bk[:], func=AF.Relu),
                         reads=[bkr], writes=[("relu", m % 3)])
                    S.op("dve", lambda e, rt=rt, m=m: e.tensor_tensor(out=hidT[:, m, :], in0=rt[:], in1=rt[:],
                                                                     op=ALU.mult),
                         reads=[("relu", m % 3)], writes=[("hid", m)])
            for m in range(KC):
                wv, wr = w_get("w2_%d_%d" % (L, m))
                bk, bkr = next_bank()
                S.op("pe", [(lambda e, k=k, wv=wv, bk=bk: e.matmul(
                    bk[:], lhsT=wv[:, k, :], rhs=hidT[:, k, :], start=(k == 0), stop=(k == 31)))
                    for k in range(32)],
                    reads=[wr] + [("hid", k) for k in range(32)], writes=[bkr])
                xa, xr = xres(m)
                S.op("dve", lambda e, m=m, bk=bk, xa=xa: e.scalar_tensor_tensor(
                    out=bufA[:, m, :], in0=bk[:], scalar=prk("G_f" + sfx, m), in1=xa,
                    op0=ALU.mult, op1=ALU.add),
                    reads=[bkr, xr, ("pr", "G_f" + sfx)], writes=[("bufA", m)])

        def preg(name):
            return "PRraw" if prmap[name][0] < 512 else ("pr", name)

        def affine_from_bufA(eng, out_fn, out_reg_fn, gname, bname):
            for m in range(KC):
                oa = out_fn(m)
                if eng == "act":
                    S.op("act", lambda e, m=m, oa=oa: e.activation(
                        out=oa, in_=bufA[:, m, :], func=AF.Identity, scale=prk(gname, m), bias=prk(bname, m)),
                        reads=[("bufA", m), preg(gname), preg(bname)], writes=[out_reg_fn(m)])
                else:
                    S.op(eng, lambda e, m=m, oa=oa: e.tensor_scalar(
                        out=oa, in0=bufA[:, m, :], scalar1=prk(gname, m), scalar2=prk(bname, m),
                        op0=ALU.mult, op1=ALU.add),
                        reads=[("bufA", m), preg(gname), preg(bname)], writes=[out_reg_fn(m)])

        x_tok = scr[:, 0:5 * 1024].rearrange("p (j f) -> p j f", j=5)
        XREG = [("a_tok", j) for j in range(5)] + [("uT", m) for m in range(4)] + [("v_tok", j) for j in range(4)]

        if debug_taps:
            tap('PR', PR[:], [128, NPAR], ['PRraw'] + [('pr', n) for n in prmap if prmap[n][0] >= 512])
            tap('mod0', mod[0][:], [128, 48, 2], [('mod', 0)])
            tap('sTc', sT_c[:], [128, KC, 2], ['sT_c'])
            tap('wp0', pieces['ada0_0'][0], [128, 8, 512], piece_regions('ada0_0'))
            tap('wp1', pieces['ada1_11'][0], [128, 8, 512], piece_regions('ada1_11'))
            tap('Chi', C_hi[:], [128, 4, 128], ['C_hi'])
            tap('Clo', C_lo[:], [128, 4, 128], ['C_lo'])
            tap('sguwT', sguwT[:], [128, 4, 128], ['sguwT'])
        def l0_block(bi, g, s, first, last):
            sfx = "0%d" % g
            xT = xTs[bi % 2]
            XT = "xT%d" % (bi % 2)
            Sg = seqs[g]
            if first:
                S.op("dve", lambda e: e.memset(x_tok[0:8, 0, :], 0.0), writes=XREG)
            if last:
                S.op("dve", lambda e: e.memset(x_tok[0:24, 4, :], 0.0), writes=XREG)
            for j in range(5):
                t0 = s - HALO0 + 128 * j
                rows = 128 if j < 4 else 24
                r0 = max(0, -t0)
                r1 = min(rows, Sg - t0)
                S.dma("sp", x_tok[r0:r1, j, :], x_in[g][t0 + r0:t0 + r1, :], writes=XREG)
            yield
            bx, bxr = next_bank()
            for k in range(KC):
                bk, bkr = next_bank()
                S.op("pe", [(lambda e, j=j, k=k, bk=bk: e.transpose(
                    bk[:, j * 128:(j + 1) * 128], x_tok[:, j, k * 128:(k + 1) * 128], ident[:])) for j in range(4)],
                    reads=XREG + ["ident"], writes=[bkr])
                S.op("act", lambda e, k=k, bk=bk: e.activation(out=xT[:, k, 0:512], in_=bk[:], func=AF.Copy),
                     reads=[bkr], writes=[(XT, k)])
                S.op("dve", lambda e, k=k: e.tensor_scalar(
                    out=hT[:, k, 0:512], in0=xT[:, k, 0:512], scalar1=prk("A_m" + sfx, k), scalar2=prk("B_m" + sfx, k),
                    op0=ALU.mult, op1=ALU.add),
                    reads=[(XT, k), ("pr", "A_m" + sfx), ("pr", "B_m" + sfx)], writes=[("hT", k)])
                yield
            S.op("pe", [(lambda e, k=k: e.transpose(bx[:, k * 24:(k + 1) * 24], x_tok[0:24, 4, k * 128:(k + 1) * 128],
                                                    ident[0:24, 0:24])) for k in range(KC)],
                 reads=XREG + ["ident"], writes=[bxr])
            S.op("act", lambda e: e.activation(out=xT[:, :, 512:536], in_=bx[:, 0:192].rearrange("p (k c) -> p k c", k=KC),
                                               func=AF.Copy), reads=[bxr], writes=[(XT, k) for k in range(KC)])
            for k in range(KC):
                S.op("dve", lambda e, k=k: e.tensor_scalar(
                    out=hT[:, k, 512:536], in0=bx[:, k * 24:(k + 1) * 24], scalar1=prk("A_m" + sfx, k),
                    scalar2=prk("B_m" + sfx, k), op0=ALU.mult, op1=ALU.add),
                    reads=[bxr, ("pr", "A_m" + sfx), ("pr", "B_m" + sfx)], writes=[("hT", k)])
            HREG = [("hT", k) for k in range(KC)]
            if debug_taps and bi == 0:
                tap("h0T", hT[:, :, 0:536], [128, KC, 536], HREG)

            yield
            wv, wr = w_get("in_0")
            for j in range(5):
                rows = 128 if j < 4 else 24
                bk, bkr = next_bank()
                S.op("pe", [(lambda e, k=k, j=j, rows=rows, bk=bk, wv=wv: e.matmul(
                    bk[0:rows, :], lhsT=hT[:, k, 128 * j:128 * j + rows], rhs=wv[:, k, :],
                    start=(k == 0), stop=(k == KC - 1))) for k in range(KC)],
                    reads=[wr] + HREG, writes=[bkr])
                S.op("act" if j % 2 == 0 else "dve",
                     (lambda e, j=j, rows=rows, bk=bk: e.activation(out=a_tok[0:rows, j, :], in_=bk[0:rows, :], func=AF.Copy))
                     if j % 2 == 0 else
                     (lambda e, j=j, rows=rows, bk=bk: e.tensor_copy(out=a_tok[0:rows, j, :], in_=bk[0:rows, :])),
                     reads=[bkr], writes=[("a_tok", j)])
                yield
            wv, wr = w_get("in_2")
            for j in range(4):
                bk, bkr = next_bank()
                c0 = HALO0 + 128 * j
                S.op("pe", [(lambda e, k=k, c0=c0, bk=bk, wv=wv: e.matmul(
                    bk[:], lhsT=hT[:, k, c0:c0 + 128], rhs=wv[:, k, :],
                    start=(k == 0), stop=(k == KC - 1))) for k in range(KC)],
                    reads=[wr] + HREG, writes=[bkr])
                S.op("act", lambda e, j=j, bk=bk: e.activation(out=v_tok[:, j, :], in_=bk[:], func=AF.Gelu),
                     reads=[bkr], writes=[("v_tok", j)])
                so = j * 16
                S.op("dve", lambda e, j=j, so=so: e.bn_stats(out=small[:, so:so + 6], in_=v_tok[:, j, :]),
                     reads=[("v_tok", j)], writes=[("small", j)])
                S.op("dve", lambda e, so=so: e.bn_aggr(out=small[:, so + 6:so + 8], in_=small[:, so:so + 6]),
                     reads=[("small", j)], writes=[("small", j)])
                S.op("act", lambda e, so=so: e.activation(out=small[:, so + 8:so + 9], in_=small[:, so + 7:so + 8],
                                                          func=AF.Sqrt, bias=eps_t[:, 0:1], scale=1.0),
                     reads=[("small", j), "eps_t"], writes=[("small", j)])
                S.op("dve", lambda e, so=so: e.reciprocal(out=small[:, so + 9:so + 10], in_=small[:, so + 8:so + 9]),
                     reads=[("small", j)], writes=[("small", j)])
                S.op("dve", lambda e, so=so: e.tensor_scalar(
                    out=small[:, so + 10:so + 11], in0=small[:, so + 6:so + 7], scalar1=small[:, so + 9:so + 10],
                    scalar2=-1.0, op0=ALU.mult, op1=ALU.mult), reads=[("small", j)], writes=[("small", j)])
                S.op("act", lambda e, j=j, so=so: e.activation(
                    out=v_tok[:, j, :], in_=v_tok[:, j, :], func=AF.Identity,
                    scale=small[:, so + 9:so + 10], bias=small[:, so + 10:so + 11]),
                    reads=[("v_tok", j), ("small", j)], writes=[("v_tok", j)])
                S.op("dve", lambda e, j=j: e.tensor_tensor(out=nG[:, j, :], in0=v_tok[:, j, :], in1=g_bc[:], op=ALU.mult),
                     reads=[("v_tok", j), "g_bc"], writes=[("nG", j)])
                yield
            yield "F1"
            wv, wr = w_get("in_1")
            for m in range(4):
                bk, bkr = next_bank()
                S.op("pe", [(lambda e, k=k, m=m, bk=bk, wv=wv: e.matmul(
                    bk[:], lhsT=wv[:, k, m * 128:(m + 1) * 128], rhs=hT[:, k, HALO0:HALO0 + T],
                    start=(k == 0), stop=(k == KC - 1))) for k in range(KC)],
                    reads=[wr] + HREG, writes=[bkr])
                S.op("act", lambda e, m=m, bk=bk: e.activation(out=uT[:, m, :], in_=bk[:], func=AF.Gelu),
                     reads=[bkr], writes=[("uT", m)])
                yield
            for gg in range(4):
                bk, bkr = next_bank()
                fns = []
                for j in range(4):
                    vm = 1 if (first and j == 0) else (2 if (last and j == 3) else 0)
                    vn = 4 if (last and j == 3) else 3
                    fns.append(lambda e, j=j, gg=gg, vm=vm, bk=bk: e.matmul(
                        bk[:, j * 128:(j + 1) * 128], lhsT=a_tok[:, j, gg * 128:(gg + 1) * 128],
                        rhs=pm_sb[:, vm, gg * 128:(gg + 1) * 128], start=True, stop=False))
                    bnd = (first and j == 0) or (last and j == 3)
                    fns.append(lambda e, j=j, gg=gg, vn=vn, bk=bk, bnd=bnd: e.matmul(
                        bk[:, j * 128:(j + 1) * 128], lhsT=a_tok[0:16, j + 1, gg * 128:(gg + 1) * 128],
                        rhs=pm_sb[0:16, vn, gg * 128:(gg + 1) * 128], start=False, stop=(not bnd)))
                    if bnd:
                        pmx = pm_x[:].rearrange("p (a g v c) -> p a g v c", a=2, g=4, v=3)
                        ext = []
                        for term in range(2):
                            if first and j == 0:
                                ext.append((bk[:, 0:8], a_tok[0:32, 0, gg * 128:(gg + 1) * 128], pmx[0:32, term, gg, 0, :]))
                            else:
                                c0_ = 3 * 128 + 120
                                ext.append((bk[:, c0_:c0_ + 8], a_tok[64:128, 3, gg * 128:(gg + 1) * 128],
                                            pmx[64:128, term, gg, 1, :]))
                                ext.append((bk[:, c0_:c0_ + 8], a_tok[0:16, 4, gg * 128:(gg + 1) * 128],
                                            pmx[0:16, term, gg, 2, :]))
                        for n_, (o_, l_, r_) in enumerate(ext):
                            fns.append(lambda e, o_=o_, l_=l_, r_=r_, lastone=(n_ == len(ext) - 1): e.matmul(
                                o_, lhsT=l_, rhs=r_, start=False, stop=lastone))
                S.op("pe", fns, reads=[("a_tok", j) for j in range(5)] + ["pm_sb"], writes=[bkr])
                S.op("act", lambda e, gg=gg, bk=bk: e.activation(out=pooledT[:, gg, :], in_=bk[:], func=AF.Copy),
                     reads=[bkr], writes=[("pooledT", gg)])
                bk2_, bk2r_ = next_bank()
                S.op("pe", [lambda e, gg=gg, bk2_=bk2_: e.matmul(bk2_[:], lhsT=poolw_b[:, gg, :], rhs=pooledT[:, gg, :],
                                                                   start=True, stop=True)],
                     reads=[("pooledT", gg), "poolw_b"], writes=[bk2r_])
                S.op("dve", lambda e, gg=gg, bk2_=bk2_: e.tensor_scalar(
                    out=yabT[:, gg, :], in0=bk2_[:], scalar1=prk("pool_scale", gg), scalar2=None, op0=ALU.mult),
                    reads=[bk2r_, "PRraw"], writes=[("yab", gg)])
                yield
            for j in range(4):
                bk, bkr = next_bank()
                fns = []
                for h in range(4):
                    fns.append(lambda e, j=j, h=h, bk=bk: e.matmul(
                        bk[:, h * 128:(h + 1) * 128], lhsT=nG[:, j, h * 128:(h + 1) * 128], rhs=sguwT[:, h, :],
                        start=True, stop=False))
                    fns.append(lambda e, h=h, bk=bk: e.matmul(
                        bk[:, h * 128:(h + 1) * 128], lhsT=ident_b[:], rhs=C_hi[:, h, :], start=False, stop=False))
                    fns.append(lambda e, h=h, bk=bk: e.matmul(
                        bk[:, h * 128:(h + 1) * 128], lhsT=ident_b[:], rhs=C_lo[:, h, :], start=False, stop=False))
                    fns.append(lambda e, h=h, bk=bk: e.matmul(
                        bk[:, h * 128:(h + 1) * 128], lhsT=ident_b[:], rhs=C_l3[:, h * 128:(h + 1) * 128],
                        start=False, stop=True))
                S.op("pe", fns, reads=[("nG", j), "sguwT", "C_hi", "C_lo", "VS", "ident_b"], writes=[bkr])
                S.op("dve", lambda e, j=j, bk=bk: e.tensor_tensor(
                    out=yabT[:, 4:8, j * 128:(j + 1) * 128], in0=bk[:].rearrange("p (h c) -> p h c", h=4),
                    in1=uT[:, :, j * 128:(j + 1) * 128], op=ALU.mult),
                    reads=[bkr] + [("uT", m) for m in range(4)], writes=[("yab", 4 + h) for h in range(4)])
                yield
            if debug_taps and bi == 0:
                tap("yabT", yabT, [128, 8, T], [("yab", m) for m in range(8)])
            yield "F"
            for jj in range(2):
                wv, wr = w_get("out_%d" % jj)
                for mm in range(4):
                    m = jj * 4 + mm
                    bk, bkr = next_bank()
                    S.op("pe", [(lambda e, k=k, mm=mm, bk=bk, wv=wv: e.matmul(
                        bk[:], lhsT=wv[:, k, mm * 128:(mm + 1) * 128], rhs=yabT[:, k, :],
                        start=(k == 0), stop=(k == KC - 1))) for k in range(KC)],
                        reads=[wr] + [("yab", k) for k in range(KC)], writes=[bkr])
                    S.op("dve", lambda e, m=m, bk=bk: e.scalar_tensor_tensor(
                        out=bufA[:, m, :], in0=bk[:], scalar=prk("G_m" + sfx, m), in1=xT[:, m, HALO0:HALO0 + T],
                        op0=ALU.mult, op1=ALU.add),
                        reads=[bkr, (XT, m), ("pr", "G_m" + sfx)], writes=[("bufA", m)])
            yield
            yield from layer_norm_fm(lambda m: ("bufA", m), 1)
            affine_from_bufA("dve", lambda m: xT[:, m, HALO0:HALO0 + T], lambda m: (XT, m), "ln1_g0", "ln1_b0")
            yield
            affine_from_bufA("act", lambda m: h1T[:, m, :], lambda m: ("h1T", m), "Gp" + sfx, "Bp" + sfx)
            yield "O"
            if debug_taps and bi == 0:
                tap("x1T", xT[:, :, HALO0:HALO0 + T], [128, KC, T], [(XT, m) for m in range(KC)])
            mlp(0, sfx, lambda m: (xT[:, m, HALO0:HALO0 + T], (XT, m)))
            yield "M"
            yield from layer_norm_fm(lambda m: ("bufA", m), 1)
            affine_from_bufA("dve", lambda m: bufA[:, m, :], lambda m: ("bufA", m), "ln2_g0", "ln2_b0")
            if debug_taps and bi == 0:
                tap("x2T", bufA[:], [128, KC, T], [("bufA", m) for m in range(KC)])
            S.dma("sp", xmid[g][:, :, s:s + T], bufA[:], reads=[("bufA", m) for m in range(KC)],
                  writes=[("xmid", g, s)])


        class G:
            def __init__(self, gen):
                self.gen = gen
                self.seen = set()
                self.done = False

            def step(self):
                if self.done:
                    return None
                try:
                    v = next(self.gen)
                except StopIteration:
                    self.done = True
                    return None
                if v is not None:
                    self.seen.add(v)
                return v

            def adv(self, marker):
                while not self.done and marker not in self.seen:
                    self.step()

        def co_adv(ga, ma, na, gb, mb, nb):
            fa = lambda: ga is None or ga.done or (ma in ga.seen)
            fb = lambda: gb is None or gb.done or (mb in gb.seen)
            while not (fa() and fb()):
                for _ in range(na):
                    if fa():
                        break
                    ga.step()
                for _ in range(nb):
                    if fb():
                        break
                    gb.step()

        def advance_until(gen, marker):
            for v in gen:
                if v == marker:
                    return

        def co_advance(ga, ma, na, gb, mb, nb):
            da = ga is None
            db = gb is None
            while not (da and db):
                for _ in range(na):
                    if da:
                        break
                    try:
                        if next(ga) == ma:
                            da = True
                    except StopIteration:
                        da = True
                for _ in range(nb):
                    if db:
                        break
                    try:
                        if next(gb) == mb:
                            db = True
                    except StopIteration:
                        db = True

        blks = blocks if DEV_STAGE >= 1 else []
        if DEV_NBLK is not None:
            blks = blks[:DEV_NBLK]
        gens = [G(l0_block(bi, *blk)) for bi, blk in enumerate(blks)]
        gg = lambda i: gens[i] if i < len(gens) else None
        if gens:
            gens[0].adv("F")
        if len(gens) > 1:
            gens[1].adv("F1")
        for i in range(len(gens)):
            co_adv(gens[i], "O", 1, gg(i + 1), "F", 2)
            if gg(i + 2) is not None:
                gg(i + 2).step()
            gens[i].adv("M")
            co_adv(gens[i], "END", 1, gg(i + 2), "F1", 2)

        do_ada(1)
        do_derived(1)
        S.barrier()

        ostage = hid_f32[:, 0:4096].rearrange("p (j f) -> p j f", j=4)
        W1C = T + 2 * HALO1
        def l1_block(bi, g, s, first, last):
            sfx = "1%d" % g
            xT = xTs[bi % 2]
            XT = "xT%d" % (bi % 2)
            Sg = seqs[g]
            XR = [(XT, k) for k in range(KC)]
            lo = s - HALO1
            hi = s + T + HALO1
            c0 = max(0, -lo)
            c1 = W1C - max(0, hi - Sg)
            if first:
                S.op("pool", lambda e: e.memset(xT[:, :, 0:HALO1], 0.0), writes=XR)
            if last:
                S.op("pool", lambda e: e.memset(xT[:, :, T + HALO1:W1C], 0.0), writes=XR)
            rds = [("xmid", g, ss) for ss in range(max(0, s - T), min(Sg, s + 2 * T), T)]
            S.dma("sp", xT[:, :, c0:c1], xmid[g][:, :, lo + c0:lo + c1], reads=rds, writes=XR)
            yield
            for k in range(KC):
                for (oc, ic, n) in [(0, HALO1, T), (T, 0, HALO1), (T + HALO1, T + HALO1, HALO1)]:
                    S.op("act", lambda e, k=k, oc=oc, ic=ic, n=n: e.activation(
                        out=hT[:, k, oc:oc + n], in_=xT[:, k, ic:ic + n], func=AF.Identity,
                        scale=prk("A_m" + sfx, k), bias=prk("B_m" + sfx, k)),
                        reads=[(XT, k), ("pr", "A_m" + sfx), ("pr", "B_m" + sfx)], writes=[("hT", k)])
            HREG = [("hT", k) for k in range(KC)]
            yield
            for k in range(KC):
                S.op("dve", lambda e, k=k: e.tensor_scalar(
                    out=xT[:, k, HALO1:HALO1 + T], in0=xT[:, k, HALO1:HALO1 + T], scalar1=prk("bg" + sfx, k),
                    scalar2=None, op0=ALU.add),
                    reads=[(XT, k), ("pr", "bg" + sfx)], writes=[(XT, k)])
            yield
            for j in range(4):
                wv, wr = w_get("pw1_%d" % j)
                for i in range(2):
                    m = j * 2 + i
                    bv, bvr = next_bank()
                    bg_, bgr = next_bank()
                    bh, bhr = next_bank()
                    fns = []
                    for k in range(KC):
                        fns.append(lambda e, k=k, i=i, wv=wv, bv=bv: e.matmul(
                            bv[:], lhsT=wv[:, k, i * 128:(i + 1) * 128], rhs=hT[:, k, 0:T],
                            start=(k == 0), stop=(k == KC - 1)))
                    for k in range(KC):
                        fns.append(lambda e, k=k, i=i, wv=wv, bg_=bg_: e.matmul(
                            bg_[:], lhsT=wv[:, k, 256 + i * 128:256 + (i + 1) * 128], rhs=hT[:, k, 0:T],
                            start=(k == 0), stop=(k == KC - 1)))
                    for k in range(KC):
                        fns.append(lambda e, k=k, i=i, wv=wv, bh=bh: e.matmul(
                            bh[:, 0:2 * HALO1], lhsT=wv[:, k, i * 128:(i + 1) * 128], rhs=hT[:, k, T:T + 2 * HALO1],
                            start=(k == 0), stop=(k == KC - 1)))
                    for k in range(KC):
                        fns.append(lambda e, k=k, i=i, wv=wv, bh=bh: e.matmul(
                            bh[:, 32:32 + 2 * HALO1], lhsT=wv[:, k, 256 + i * 128:256 + (i + 1) * 128],
                            rhs=hT[:, k, T:T + 2 * HALO1], start=(k == 0), stop=(k == KC - 1)))
                    S.op("pe", fns, reads=[wr] + HREG, writes=[bvr, bgr, bhr])
                    sg = sig[:, m % 2, :]
                    S.op("act", lambda e, m=m, sg=sg, bg_=bg_: e.activation(
                        out=sg[:, 0:T], in_=bg_[:], func=AF.Sigmoid, bias=prk("pw1_b", 8 + m), scale=1.0),
                        reads=[bgr, "PRraw"], writes=[("sig", m % 2)])
                    S.op("act", lambda e, m=m, sg=sg, bh=bh: e.activation(
                        out=sg[:, T:T + 2 * HALO1], in_=bh[:, 32:32 + 2 * HALO1], func=AF.Sigmoid,
                        bias=prk("pw1_b", 8 + m), scale=1.0),
                        reads=[bhr, "PRraw", ("sig", m % 2)], writes=[("sig", m % 2)])
                    S.op("dve", lambda e, m=m, sg=sg, bv=bv: e.scalar_tensor_tensor(
                        out=gT[:, m, HALO1:HALO1 + T], in0=bv[:], scalar=prk("pw1_b", m), in1=sg[:, 0:T],
                        op0=ALU.add, op1=ALU.mult),
                        reads=[bvr, ("sig", m % 2), "PRraw"], writes=[("gT", m)])
                    S.op("dve", lambda e, m=m, sg=sg, bh=bh: e.scalar_tensor_tensor(
                        out=gT[:, m, 0:HALO1], in0=bh[:, 0:HALO1], scalar=prk("pw1_b", m), in1=sg[:, T:T + HALO1],
                        op0=ALU.add, op1=ALU.mult),
                        reads=[bhr, ("sig", m % 2), "PRraw", ("gT", m)], writes=[("gT", m)])
                    S.op("dve", lambda e, m=m, sg=sg, bh=bh: e.scalar_tensor_tensor(
                        out=gT[:, m, T + HALO1:W1C], in0=bh[:, HALO1:2 * HALO1], scalar=prk("pw1_b", m),
                        in1=sg[:, T + HALO1:T + 2 * HALO1], op0=ALU.add, op1=ALU.mult),
                        reads=[bhr, ("sig", m % 2), "PRraw", ("gT", m)], writes=[("gT", m)])
                    if first:
                        S.op("pool", lambda e, m=m: e.memset(gT[:, m, 0:HALO1], 0.0), reads=[("gT", m)], writes=[("gT", m)])
                    if last:
                        S.op("pool", lambda e, m=m: e.memset(gT[:, m, T + HALO1:W1C], 0.0), reads=[("gT", m)],
                             writes=[("gT", m)])
                    yield
            yield "F"
            def build_dg(m):
                dgs_ = dg[:, m % 2, :].rearrange("p (t c) -> p t c", t=TAPS)
                S.op("pool" if m % 2 == 0 else "dve", lambda e, m=m, dgs_=dgs_: e.tensor_tensor(
                    out=dgs_, in0=ident_b[:].unsqueeze(1).to_broadcast([128, TAPS, 128]),
                    in1=wc[:, m, :].unsqueeze(2).to_broadcast([128, TAPS, 128]), op=ALU.mult),
                    reads=["ident_b", "PRraw"], writes=[("dg", m % 2)])

            build_dg(0)
            build_dg(1)
            yield "D"
            conv_banks = []
            for m in range(KC):
                dgs = dg[:, m % 2, :].rearrange("p (t c) -> p t c", t=TAPS)
                bk, bkr = next_bank()
                S.op("pe", [(lambda e, t=t, m=m, dgs=dgs, bk=bk: e.matmul(
                    bk[:], lhsT=dgs[:, t, :], rhs=gT[:, m, t:t + T], start=(t == 0), stop=(t == TAPS - 1)))
                    for t in range(TAPS)],
                    reads=[("dg", m % 2), ("gT", m)], writes=[bkr])
                conv_banks.append((bk, bkr))
                if m + 2 < KC:
                    build_dg(m + 2)
                yield
            yield "V"
            for m in range(KC):
                bk, bkr = conv_banks[m]
                S.op("act", lambda e, m=m, bk=bk: e.activation(out=bufA[:, m, :], in_=bk[:], func=AF.Identity,
                                                             bias=prk("dw_b", m), scale=1.0),
                     reads=[bkr, "PRraw"], writes=[("bufA", m)])
            yield "E"
            if debug_taps and bi == 0:
                tap("dT", bufA[:], [128, KC, T], [("bufA", m) for m in range(KC)])
            yield from layer_norm_fm(lambda m: ("bufA", m), 0)
            for m in range(KC):
                S.op("act", lambda e, m=m: e.activation(out=sT[:, m, :], in_=bufA[:, m, :], func=AF.Silu,
                                                        scale=prk("cn_g", m), bias=prk("cn_b", m)),
                     reads=[("bufA", m), "PRraw", "PRraw"], writes=[("sT", m)])
            yield
            for jj in range(2):
                wv, wr = w_get("pw2_%d" % jj)
                for mm in range(4):
                    m = jj * 4 + mm
                    bk, bkr = next_bank()
                    S.op("pe", [(lambda e, k=k, mm=mm, bk=bk, wv=wv: e.matmul(
                        bk[:], lhsT=wv[:, k, mm * 128:(mm + 1) * 128], rhs=sT[:, k, :],
                        start=(k == 0), stop=(k == KC - 1))) for k in range(KC)],
                        reads=[wr] + [("sT", k) for k in range(KC)], writes=[bkr])
                    S.op("dve", lambda e, m=m, bk=bk: e.scalar_tensor_tensor(
                        out=bufA[:, m, :], in0=bk[:], scalar=prk("G_m" + sfx, m), in1=xT[:, m, HALO1:HALO1 + T],
                        op0=ALU.mult, op1=ALU.add),
                        reads=[bkr, (XT, m), ("pr", "G_m" + sfx)], writes=[("bufA", m)])
                    yield
            yield from layer_norm_fm(lambda m: ("bufA", m), 1)
            affine_from_bufA("dve", lambda m: xT[:, m, HALO1:HALO1 + T], lambda m: (XT, m), "ln1_g1", "ln1_b1")
            yield
            affine_from_bufA("act", lambda m: h1T[:, m, :], lambda m: ("h1T", m), "Gp" + sfx, "Bp" + sfx)
            yield "P"
            mlp(1, sfx, lambda m: (xT[:, m, HALO1:HALO1 + T], (XT, m)))
            yield "M2"
            yield from layer_norm_fm(lambda m: ("bufA", m), 1)
            affine_from_bufA("dve", lambda m: xT[:, m, HALO1:HALO1 + T], lambda m: (XT, m), "ln2_g1", "ln2_b1")
            yield "L"
            OREG = [("hid", m) for m in range(32)]
            for j in range(4):
                for half in range(2):
                    bk, bkr = next_bank()
                    S.op("pe", [(lambda e, kk=kk, j=j, half=half, bk=bk: e.transpose(
                        bk[:, kk * 128:(kk + 1) * 128], xT[:, half * 4 + kk, HALO1 + j * 128:HALO1 + (j + 1) * 128],
                        ident[:])) for kk in range(4)],
                        reads=[(XT, half * 4 + kk) for kk in range(4)] + ["ident"], writes=[bkr])
                    S.op("act" if half == 0 else "dve",
                         (lambda e, j=j, half=half, bk=bk: e.activation(
                             out=ostage[:, j, half * 512:(half + 1) * 512], in_=bk[:], func=AF.Copy))
                         if half == 0 else
                         (lambda e, j=j, half=half, bk=bk: e.tensor_copy(
                             out=ostage[:, j, half * 512:(half + 1) * 512], in_=bk[:])),
                         reads=[bkr], writes=OREG)
            S.dma("sp", y_out[g][s:s + T, :].rearrange("(j p) f -> p j f", p=128), ostage, reads=OREG)


        blks = blocks if DEV_STAGE >= 2 else []
        if DEV_NBLK is not None:
            blks = blks[:DEV_NBLK]
        gens = [G(l1_block(bi, *blk)) for bi, blk in enumerate(blks)]
        gg = lambda i: gens[i] if i < len(gens) else None
        if gens:
            gens[0].adv("F")
        for i in range(len(gens)):
            gens[i].adv("E")
            co_adv(gens[i], "P", 2, gg(i + 1), "F", 1)
            gens[i].adv("M2")
            if gg(i + 1) is not None:
                gens[i + 1].adv("D")
            co_adv(gens[i], "L", 1, gg(i + 1), "V", 1)
            if gg(i + 1) is not None:
                gens[i + 1].adv("E")
            gens[i].adv("END")

        if collect:
            return list(wstate["order"])
        S.finish()
        S.emit()
    return nc, taps, S


_PROG_CACHE = {}


def _run(inputs, seq_p, seq_s, n_cores=8, debug_taps=False):
    key = (seq_p, seq_s, debug_taps)
    if key not in _PROG_CACHE:
        _PROG_CACHE[key] = build_program(seq_p, seq_s, debug_taps)
    nc, taps, S = _PROG_CACHE[key]
    pm_p = np.ascontiguousarray(_pool_mats(seq_p))
    pm_s = np.ascontiguousarray(_pool_mats(seq_s))
    pm_x_host = np.ascontiguousarray(_pool_resid(seq_p))
    in_maps = []
    for c in range(n_cores):
        m = {
            "x_p": np.ascontiguousarray(inputs["x_prompt"][c, :seq_p]),
            "x_s": np.ascontiguousarray(inputs["x_sample"][c, :seq_s]),
            "c": np.ascontiguousarray(np.stack([inputs["c_prompt"][c], inputs["c_sample"][c]], 0)),
            "pmat_p": pm_p, "pmat_s": pm_s, "pmat_x": pm_x_host,
        }
        for n in WEIGHT_NAMES:
            m[n] = np.ascontiguousarray(inputs[n])
        in_maps.append(m)
    res = run_bass_kernel_spmd(nc, in_maps, core_ids=list(range(n_cores)))
    return res


def kernel(**inputs):
    inputs = {k: np.asarray(v) for k, v in inputs.items()}
    seq_p = inputs["x_prompt"].shape[1]
    seq_s = inputs["x_sample"].shape[1]
    res = _run(inputs, seq_p, seq_s, n_cores=8)
    y_p = np.stack([np.asarray(r["y_p"]) for r in res.results], 0).astype(np.float32)
    y_s = np.stack([np.asarray(r["y_s"]) for r in res.results], 0).astype(np.float32)
    return (y_p, y_s)
```
